# Optimizing a Trainium2 kernel written in Bass

```python
import math
import jax, jax.numpy as jnp
from jax import lax
import numpy as np

D_MODEL = 2048
BATCH = 4
SEQ = 4096
DEPTH = 4

GRID_W = 64
CTX_LEN = 256
N_MIXERS = 4
N_HEADS = 16
HEAD_DIM = D_MODEL // N_HEADS
N_KV_HEADS = 4
NA_WIN_R = 8
NA_WIN_C = 16
CONV_WIDTH = 31
SHORT_CONV_WIDTH = 3
SWA_WINDOW = 128
SWA_BLOCK = 128
D_FF = ((8 * D_MODEL // 3 + 255) // 256) * 256
ROPE_BASE = 10000.0
ROPE_AXIS_DIM = HEAD_DIM // 2
N_MOD = 9
MACARON_WEIGHT = 0.5
EPS = 1e-6
NEG_INF = -1e30
ATTN_KINDS = (0, 3)
N_NA = (DEPTH + 3) // 4
N_CV = (DEPTH + 2) // 4
N_SC = (DEPTH + 1) // 4
N_SWA = DEPTH // 4

kernel_name = "hybrid_dit_na_conformer_shortconv_swa"


def rms_norm(x, g):
    xf = x.astype(jnp.float32)
    y = xf * lax.rsqrt(jnp.mean(xf * xf, axis=-1, keepdims=True) + EPS)
    return (y * g.astype(jnp.float32)).astype(x.dtype)


def layer_norm(x, g, b):
    xf = x.astype(jnp.float32)
    mu = jnp.mean(xf, axis=-1, keepdims=True)
    var = jnp.mean(jnp.square(xf - mu), axis=-1, keepdims=True)
    y = (xf - mu) * lax.rsqrt(var + EPS)
    return (y * g.astype(jnp.float32) + b.astype(jnp.float32)).astype(x.dtype)


def modulate(h, shift, scale):
    return h * (1 + scale) + shift


def swiglu(h, w_in, w_out):
    gate, up = jnp.split(h @ w_in, 2, axis=-1)
    return (jax.nn.silu(gate) * up) @ w_out


def ffn_sub(h, shift, scale, gate, g, w_in, w_out):
    return h + MACARON_WEIGHT * gate * swiglu(modulate(rms_norm(h, g), shift, scale), w_in, w_out)


def depthwise_conv(h, w, b=None):
    k = w.shape[0]
    pad = k // 2
    y = lax.conv_general_dilated(h, w[:, None, :].astype(h.dtype), window_strides=(1,),
                                 padding=[(pad, pad)], dimension_numbers=("NWC", "WIO", "NWC"),
                                 feature_group_count=h.shape[-1])
    return y if b is None else y + b


def rope_axis(x, ang):
    d2 = x.shape[-1] // 2
    x1, x2 = x[..., :d2], x[..., d2:]
    cos = jnp.cos(ang)[:, None, :]
    sin = jnp.sin(ang)[:, None, :]
    return jnp.concatenate([x1 * cos - x2 * sin, x1 * sin + x2 * cos], axis=-1).astype(x.dtype)


def rope_2d(x, ang_r, ang_c):
    half = x.shape[-1] // 2
    return jnp.concatenate([rope_axis(x[..., :half], ang_r), rope_axis(x[..., half:], ang_c)], axis=-1)


def context_self_attention(qc, kc, vc, sink):
    B, L, KV, G, _ = qc.shape
    s = jnp.einsum('bqkgd,blkd->bkgql', qc, kc).astype(jnp.float32)
    if sink is not None:
        s_sink = jnp.broadcast_to(sink.reshape(KV, G)[None, :, :, None, None].astype(jnp.float32), (B, KV, G, L, 1))
        s = jnp.concatenate([s, s_sink], axis=-1)
    p = jax.nn.softmax(s, axis=-1)[..., :L].astype(vc.dtype)
    o = jnp.einsum('bkgql,blkd->bqkgd', p, vc)
    return o.reshape(B, L, KV * G * qc.shape[-1])


def neighbourhood_attention(u, uc, w_qkv, w_o, rpb, ctx_out):
    B, S, D = u.shape
    L = uc.shape[1]
    rows = S // GRID_W
    wr = min(NA_WIN_R, rows)
    wc = NA_WIN_C
    scale = HEAD_DIM ** -0.5
    qkv = (u @ w_qkv).reshape(B, S, 3, N_HEADS, HEAD_DIM)
    qkv_c = (uc @ w_qkv).reshape(B, L, 3, N_HEADS, HEAD_DIM)
    kc, vc = qkv_c[:, :, 1], qkv_c[:, :, 2]
    qg = (qkv[:, :, 0] * scale).reshape(B, rows, GRID_W, N_HEADS, HEAD_DIM)
    kg = qkv[:, :, 1].reshape(B, rows, GRID_W, N_HEADS, HEAD_DIM)
    vg = qkv[:, :, 2].reshape(B, rows, GRID_W, N_HEADS, HEAD_DIM)
    row_start = jnp.clip(jnp.arange(rows) - wr // 2, 0, rows - wr)
    col_start = jnp.clip(jnp.arange(GRID_W) - wc // 2, 0, GRID_W - wc)
    col_idx = col_start[:, None] + jnp.arange(wc)[None, :]
    col_off = col_idx - jnp.arange(GRID_W)[:, None] + (NA_WIN_C - 1)
    rpb_c = rpb[:, :, col_off]
    n_loc = wr * wc

    def row_block(r):
        rs = row_start[r]
        q_r = lax.dynamic_index_in_dim(qg, r, axis=1, keepdims=False)
        k_rows = lax.dynamic_slice_in_dim(kg, rs, wr, axis=1)
        v_rows = lax.dynamic_slice_in_dim(vg, rs, wr, axis=1)
        k_win = jnp.take(k_rows, col_idx, axis=2)
        v_win = jnp.take(v_rows, col_idx, axis=2)
        s_loc = jnp.einsum('bqhd,brqkhd->bhqrk', q_r, k_win).astype(jnp.float32)
        row_off = rs + jnp.arange(wr) - r + (NA_WIN_R - 1)
        bias = jnp.take(rpb_c, row_off, axis=1).transpose(0, 2, 1, 3)
        s_loc = (s_loc + bias[None].astype(jnp.float32)).reshape(B, N_HEADS, GRID_W, n_loc)
        s_ctx = jnp.einsum('bqhd,blhd->bhql', q_r, kc).astype(jnp.float32)
        p = jax.nn.softmax(jnp.concatenate([s_loc, s_ctx], axis=-1), axis=-1).astype(vg.dtype)
        p_loc = p[..., :n_loc].reshape(B, N_HEADS, GRID_W, wr, wc)
        p_ctx = p[..., n_loc:]
        return (jnp.einsum('bhqrk,brqkhd->bqhd', p_loc, v_win)
                + jnp.einsum('bhql,blhd->bqhd', p_ctx, vc))

    o = lax.map(row_block, jnp.arange(rows))
    y = o.transpose(1, 0, 2, 3, 4).reshape(B, S, D) @ w_o
    yc = None
    if ctx_out:
        qc = (qkv_c[:, :, 0] * scale).reshape(B, L, N_HEADS, 1, HEAD_DIM)
        yc = context_self_attention(qc, kc, vc, None) @ w_o
    return y, yc


def conformer_conv(h, w_pw1, b_pw1, w_dw, b_dw, ln_g, ln_b, w_pw2, b_pw2):
    a, gt = jnp.split(h @ w_pw1 + b_pw1, 2, axis=-1)
    z = a * jax.nn.sigmoid(gt)
    z = depthwise_conv(z, w_dw, b_dw)
    z = jax.nn.silu(layer_norm(z, ln_g, ln_b))
    return z @ w_pw2 + b_pw2


def short_gated_conv(h, w_in, w_conv, w_out):
    bg, cg, xin = jnp.split(h @ w_in, 3, axis=-1)
    return (bg * depthwise_conv(cg * xin, w_conv)) @ w_out


def windowed_gqa_sink(u, uc, w_qkv, w_o, sink, ang_r, ang_c, ctx_out):
    B, S, D = u.shape
    L = uc.shape[1]
    G = N_HEADS // N_KV_HEADS
    kvd = N_KV_HEADS * HEAD_DIM
    scale = HEAD_DIM ** -0.5

    def split_qkv(h, n):
        z = h @ w_qkv
        q = z[..., :D].reshape(B, n, N_HEADS, HEAD_DIM)
        k = z[..., D:D + kvd].reshape(B, n, N_KV_HEADS, HEAD_DIM)
        v = z[..., D + kvd:].reshape(B, n, N_KV_HEADS, HEAD_DIM)
        return q, k, v

    q, k, v = split_qkv(u, S)
    q = rope_2d(q, ang_r, ang_c) * scale
    k = rope_2d(k, ang_r, ang_c)
    qc, kc, vc = split_qkv(uc, L)
    nb = S // SWA_BLOCK
    qb = q.reshape(B, nb, SWA_BLOCK, N_KV_HEADS, G, HEAD_DIM)
    pad = ((0, 0), (SWA_BLOCK, SWA_BLOCK), (0, 0), (0, 0))
    kp = jnp.pad(k, pad).reshape(B, nb + 2, SWA_BLOCK, N_KV_HEADS, HEAD_DIM)
    vp = jnp.pad(v, pad).reshape(B, nb + 2, SWA_BLOCK, N_KV_HEADS, HEAD_DIM)
    kw = jnp.concatenate([kp[:, :-2], kp[:, 1:-1], kp[:, 2:]], axis=2)
    vw = jnp.concatenate([vp[:, :-2], vp[:, 1:-1], vp[:, 2:]], axis=2)
    qpos = jnp.arange(nb)[:, None] * SWA_BLOCK + jnp.arange(SWA_BLOCK)[None, :]
    kpos = jnp.arange(nb)[:, None] * SWA_BLOCK - SWA_BLOCK + jnp.arange(3 * SWA_BLOCK)[None, :]
    mask = ((jnp.abs(kpos[:, None, :] - qpos[:, :, None]) <= SWA_WINDOW)
            & (kpos[:, None, :] >= 0) & (kpos[:, None, :] < S))
    s_loc = jnp.einsum('bnqkgd,bnjkd->bnkgqj', qb, kw).astype(jnp.float32)
    s_loc = jnp.where(mask[None, :, None, None], s_loc, NEG_INF)
    s_ctx = jnp.einsum('bnqkgd,blkd->bnkgql', qb, kc).astype(jnp.float32)
    s_sink = jnp.broadcast_to(sink.reshape(N_KV_HEADS, G)[None, None, :, :, None, None].astype(jnp.float32),
                              (B, nb, N_KV_HEADS, G, SWA_BLOCK, 1))
    p = jax.nn.softmax(jnp.concatenate([s_loc, s_ctx, s_sink], axis=-1), axis=-1).astype(v.dtype)
    n_loc = 3 * SWA_BLOCK
    o = (jnp.einsum('bnkgqj,bnjkd->bnqkgd', p[..., :n_loc], vw)
         + jnp.einsum('bnkgql,blkd->bnqkgd', p[..., n_loc:n_loc + L], vc))
    y = o.reshape(B, S, D) @ w_o
    yc = None
    if ctx_out:
        qcg = (qc * scale).reshape(B, L, N_KV_HEADS, G, HEAD_DIM)
        yc = context_self_attention(qcg, kc, vc, sink) @ w_o
    return y, yc


def setup_inputs(seed: int = 0) -> dict:
    key = jax.random.key(seed)
    ks = jax.random.split(key, 27)
    D, F, H = D_MODEL, D_FF, N_HEADS
    kvd = N_KV_HEADS * HEAD_DIM

    def nrm(k, shape, s):
        return jax.random.normal(k, shape, jnp.float32) * s

    return {
        "x": nrm(ks[0], (BATCH, SEQ, D), 1.0),
        "c": nrm(ks[1], (BATCH, D), 1.0),
        "ctx": nrm(ks[2], (BATCH, CTX_LEN, D), 1.0),
        "c_ctx": nrm(ks[3], (D,), 1.0),
        "w_mod": nrm(ks[4], (DEPTH, D, N_MOD * D), D ** -0.5),
        "b_mod": nrm(ks[5], (DEPTH, N_MOD * D), 0.02),
        "norm_g": 1.0 + nrm(ks[6], (DEPTH, 3, D), 0.02),
        "ffn_w_in": nrm(ks[7], (DEPTH, 2, D, 2 * F), D ** -0.5),
        "ffn_w_out": nrm(ks[8], (DEPTH, 2, F, D), F ** -0.5),
        "na_w_qkv": nrm(ks[9], (N_NA, D, 3 * D), D ** -0.5),
        "na_w_o": nrm(ks[10], (N_NA, D, D), D ** -0.5),
        "na_rpb": nrm(ks[11], (N_NA, H, 2 * NA_WIN_R - 1, 2 * NA_WIN_C - 1), 0.1),
        "cv_w_pw1": nrm(ks[12], (N_CV, D, 2 * D), D ** -0.5),
        "cv_b_pw1": nrm(ks[13], (N_CV, 2 * D), 0.02),
        "cv_w_dw": nrm(ks[14], (N_CV, CONV_WIDTH, D), CONV_WIDTH ** -0.5),
        "cv_b_dw": nrm(ks[15], (N_CV, D), 0.02),
        "cv_ln_g": 1.0 + nrm(ks[16], (N_CV, D), 0.02),
        "cv_ln_b": nrm(ks[17], (N_CV, D), 0.02),
        "cv_w_pw2": nrm(ks[18], (N_CV, D, D), D ** -0.5),
        "cv_b_pw2": nrm(ks[19], (N_CV, D), 0.02),
        "sc_w_in": nrm(ks[20], (N_SC, D, 3 * D), D ** -0.5),
        "sc_w_conv": nrm(ks[21], (N_SC, SHORT_CONV_WIDTH, D), SHORT_CONV_WIDTH ** -0.5),
        "sc_w_out": nrm(ks[22], (N_SC, D, D), D ** -0.5),
        "swa_w_qkv": nrm(ks[23], (N_SWA, D, D + 2 * kvd), D ** -0.5),
        "swa_w_o": nrm(ks[24], (N_SWA, D, D), D ** -0.5),
        "swa_sink": nrm(ks[25], (N_SWA, H), 1.0),
        "final_g": 1.0 + nrm(ks[26], (D,), 0.02),
    }


def reference(x, c, ctx, c_ctx, w_mod, b_mod, norm_g, ffn_w_in, ffn_w_out,
              na_w_qkv, na_w_o, na_rpb,
              cv_w_pw1, cv_b_pw1, cv_w_dw, cv_b_dw, cv_ln_g, cv_ln_b, cv_w_pw2, cv_b_pw2,
              sc_w_in, sc_w_conv, sc_w_out,
              swa_w_qkv, swa_w_o, swa_sink, final_g):
    B, S, D = x.shape
    t = jnp.arange(S)
    inv_freq = jnp.power(ROPE_BASE, -jnp.arange(ROPE_AXIS_DIM // 2, dtype=jnp.float32) / (ROPE_AXIS_DIM // 2))
    ang_r = (t // GRID_W).astype(jnp.float32)[:, None] * inv_freq[None, :]
    ang_c = (t % GRID_W).astype(jnp.float32)[:, None] * inv_freq[None, :]
    silu_c = jax.nn.silu(c)
    silu_cc = jax.nn.silu(c_ctx)

    for i in range(DEPTH):
        kind = i % N_MIXERS
        j = i // N_MIXERS
        last = i == DEPTH - 1
        ml = (silu_c @ w_mod[i] + b_mod[i]).reshape(B, N_MOD, D)[:, :, None, :]
        mc = (silu_cc @ w_mod[i] + b_mod[i]).reshape(N_MOD, D)
        g = norm_g[i]
        need_ctx = (not last) or (kind in ATTN_KINDS)

        x = ffn_sub(x, ml[:, 0], ml[:, 1], ml[:, 2], g[0], ffn_w_in[i, 0], ffn_w_out[i, 0])
        if need_ctx:
            ctx = ffn_sub(ctx, mc[0], mc[1], mc[2], g[0], ffn_w_in[i, 0], ffn_w_out[i, 0])

        u = modulate(rms_norm(x, g[1]), ml[:, 3], ml[:, 4])
        uc = modulate(rms_norm(ctx, g[1]), mc[3], mc[4]) if need_ctx else None
        if kind == 0:
            y, yc = neighbourhood_attention(u, uc, na_w_qkv[j], na_w_o[j], na_rpb[j], not last)
        elif kind == 1:
            cv = (cv_w_pw1[j], cv_b_pw1[j], cv_w_dw[j], cv_b_dw[j], cv_ln_g[j], cv_ln_b[j], cv_w_pw2[j], cv_b_pw2[j])
            y = conformer_conv(u, *cv)
            yc = None if last else conformer_conv(uc, *cv)
        elif kind == 2:
            y = short_gated_conv(u, sc_w_in[j], sc_w_conv[j], sc_w_out[j])
            yc = None if last else short_gated_conv(uc, sc_w_in[j], sc_w_conv[j], sc_w_out[j])
        else:
            y, yc = windowed_gqa_sink(u, uc, swa_w_qkv[j], swa_w_o[j], swa_sink[j], ang_r, ang_c, not last)
        x = x + ml[:, 5] * y

        x = ffn_sub(x, ml[:, 6], ml[:, 7], ml[:, 8], g[2], ffn_w_in[i, 1], ffn_w_out[i, 1])
        if not last:
            ctx = ctx + mc[5] * yc
            ctx = ffn_sub(ctx, mc[6], mc[7], mc[8], g[2], ffn_w_in[i, 1], ffn_w_out[i, 1])

    return rms_norm(x, final_g)
```

```python
import numpy as np
import ml_dtypes
import concourse.bass as bass
import concourse.mybir as mybir
from concourse.bass_utils import run_bass_kernel_spmd

F32 = mybir.dt.float32
BF16 = mybir.dt.bfloat16
AF = mybir.ActivationFunctionType
ALU = mybir.AluOpType

D = 2048
KC = 16
FF = 5632
FC = 44
WIN = 2560
NCTX = 256
TT = WIN + NCTX
GRID_W = 64
EPS = 1e-6
SCALE = 128 ** -0.5
NEG = -30000.0
TILES = [(0, 512, 0), (512, 512, 0), (1024, 512, 0), (1536, 512, 0), (2048, 512, 0), (2560, 256, 1)]
NBLK = TT // 128
WSLOT = 5632
NWB = 4


class Res:
    __slots__ = ("w", "r", "sem", "excl")

    def __init__(self, sem=None):
        self.w = None
        self.r = {}
        self.sem = sem
        self.excl = False


class Prog:
    ENG = ("pe", "act", "dve", "pool", "sp")

    def __init__(self):
        self.ops = {e: [] for e in self.ENG}
        self.cnt = {e: 0 for e in self.ENG}
        self.known = {e: {} for e in self.ENG}
        self.dcnt = {}
        self.dry = False
        self.nsem = 0
        self.free_sems = []
        self.phase_sems = []
        self.nobarrier = set()

    def res(self, dma=False, persistent=False):
        r = Res()
        if dma and not self.dry:
            if self.free_sems:
                name = self.free_sems.pop()
            else:
                name = "d%d" % self.nsem
                self.nsem += 1
                self.dcnt[name] = 0
            r.sem = name
            if not persistent:
                self.phase_sems.append(name)
        return r

    def release_phase(self):
        self.free_sems.extend(self.phase_sems)
        self.phase_sems = []

    def _need(self, eng, waits, s, v):
        if s == "pe" and eng == "pe":
            return
        if self.known[eng].get(s, 0) >= v:
            return
        if waits.get(s, 0) < v:
            waits[s] = v

    def _deps(self, eng, reads, writes):
        waits = {}
        for r in reads:
            if r.w is not None:
                self._need(eng, waits, *r.w)
        for w in writes:
            if w.w is not None:
                self._need(eng, waits, *w.w)
            for s, v in w.r.items():
                self._need(eng, waits, s, v)
        for s, v in waits.items():
            self.known[eng][s] = v
        return tuple(waits.items())

    def _mark(self, tag, reads, writes):
        s, v = tag
        for r in reads:
            if r.r.get(s, 0) < v:
                r.r[s] = v
        for w in writes:
            w.w = tag
            w.r = {}

    def op(self, eng, fn, reads=(), writes=()):
        if self.dry:
            return
        ex = tuple(r for r in reads if r.excl)
        if ex:
            writes = tuple(writes) + ex
            reads = tuple(r for r in reads if not r.excl)
        waits = self._deps(eng, reads, writes)
        self.cnt[eng] += 1
        self.ops[eng].append((waits, fn, (eng, 1)))
        self._mark((eng, self.cnt[eng]), reads, writes)

    def dma(self, q, fn, sres, reads=(), writes=()):
        if self.dry:
            return
        sem = sres.sem
        waits = dict(self._deps(q, reads, writes))
        if self.dcnt[sem] > 0:
            self._need(q, waits, sem, self.dcnt[sem])
            self.known[q][sem] = max(self.known[q].get(sem, 0), self.dcnt[sem])
        self.dcnt[sem] += 16
        self.ops[q].append((tuple(waits.items()), fn, (sem, 16)))
        self._mark((sem, self.dcnt[sem]), reads, writes)

    def barrier(self):
        if self.dry:
            return
        tot = dict(self.cnt)
        tot.update(self.dcnt)
        tot.pop("pool", None)
        for sname in self.nobarrier:
            tot.pop(sname, None)
        for e in self.ENG:
            if e == "pool":
                continue
            waits = {}
            for s, v in tot.items():
                if v > 0 and s != e:
                    self._need(e, waits, s, v)
            for s, v in waits.items():
                self.known[e][s] = v
            if waits:
                self.ops[e].append((tuple(waits.items()), None, None))

    def emit(self, nc, es):
        names = list(self.ENG) + list(self.dcnt.keys())
        sems = {n: es.enter_context(nc.semaphore("s_" + n)) for n in names}
        block = es.enter_context(nc.Block())

        def mk(en):
            def body(e):
                for waits, fn, inc in self.ops[en]:
                    for s, v in waits:
                        e.wait_ge(sems[s], v)
                    if fn is not None:
                        fn(e).then_inc(sems[inc[0]], inc[1])
            return body

        block.tensor(mk("pe"))
        block.scalar(mk("act"))
        block.vector(mk("dve"))
        block.gpsimd(mk("pool"))
        block.sync(mk("sp"))


class Builder:
    def __init__(self, stop_after=None, dbg=False, lo=0, hi=4):
        self.lo, self.hi = lo, hi
        nl = hi - lo
        self.stop_after = stop_after
        self.dbg = dbg
        nc = self.nc = bass.Bass("TRN2", target_bir_lowering=False)
        P = self.P = Prog()

        def din(name, shape, dt=F32):
            return nc.dram_tensor(name, list(shape), dt, kind="ExternalInput").ap()

        if lo == 0:
            self.x_win = din("x_win", [WIN, D])
            self.ctx_in = din("ctx_in", [NCTX, D])
        else:
            self.xt_in = din("xt_in", [D, TT])
        self.cvec = din("cvec", [2, D])
        self.w_mod = din("w_mod", [nl, D, 9 * D])
        self.b_mod = din("b_mod", [nl, 9 * D])
        self.norm_g = din("norm_g", [nl, 3, D])
        self.ffn_w_in = din("ffn_w_in", [nl, 2, D, 2 * FF])
        self.ffn_w_out = din("ffn_w_out", [nl, 2, FF, D])
        if lo <= 0 < hi:
            self.na_w_qkv = din("na_w_qkv", [D, 3 * D])
            self.na_w_o = din("na_w_o", [D, D])
            self.na_bias = din("na_bias", [16, 128, 25 * 128])
        if lo <= 1 < hi:
          self.cv_w_pw1 = din("cv_w_pw1", [D, 2 * D])
          self.cv_b_pw1 = din("cv_b_pw1", [2 * D])
          self.cv_w_dw = din("cv_w_dw", [31, D])
          self.cv_b_dw = din("cv_b_dw", [D])
          self.cv_ln_g = din("cv_ln_g", [D])
          self.cv_ln_b = din("cv_ln_b", [D])
          self.cv_w_pw2 = din("cv_w_pw2", [D, D])
          self.cv_b_pw2 = din("cv_b_pw2", [D])
        if lo <= 2 < hi:
            self.sc_w_in = din("sc_w_in", [D, 3 * D])
            self.sc_w_conv = din("sc_w_conv", [3, D])
            self.sc_w_out = din("sc_w_out", [D, D])
        if lo <= 3 < hi:
            self.swa_w_qkv = din("swa_w_qkv", [D, 3072])
            self.swa_w_sw = din("swa_w_sw", [D, 2560])
            self.swa_w_o = din("swa_w_o", [D, D])
            self.swa_sink = din("swa_sink", [128, 16])
            self.rope_cos = din("rope_cos", [128, WIN])
            self.rope_sin = din("rope_sin", [128, WIN])
            self.swa_mask = din("swa_mask", [128, 256])
        self.ident_in = din("ident", [128, 128])
        if hi == 4:
            self.final_g = din("final_g", [D])
            self.out = nc.dram_tensor("out", [2048, D], F32, kind="ExternalOutput").ap()
        else:
            self.xt_out = nc.dram_tensor("xt_out", [D, TT], F32, kind="ExternalOutput").ap()
        if dbg:
            self.dbg_xt = nc.dram_tensor("dbg_xt", [D, TT], F32, kind="ExternalOutput").ap()

        def dscr(name, shape, dt):
            return nc.dram_tensor(name, list(shape), dt, kind="Internal").ap()

        self.XT = dscr("XT", [D, TT], F32)
        self.CT = dscr("CT", [D, TT], F32)
        self.ST = dscr("ST", [D, TT], BF16)
        self.QT = dscr("QT", [D, TT], BF16)
        self.KT = dscr("KT", [D, TT], BF16)
        self.VV = dscr("VV", [TT, D], BF16)
        self.xt_res = [P.res() for _ in TILES]
        self.ct_res = [P.res() for _ in range(KC)]
        self.st_res = [P.res() for _ in range(KC)]
        self.qt_res = [P.res() for _ in range(KC)]
        self.kt_res = [P.res() for _ in range(KC)]
        self.vv_res = [P.res() for _ in range(NBLK)]
        self.out_res = P.res()

        self.wring_t = nc.alloc_sbuf_tensor("wring", [128, NWB * WSLOT], BF16)
        self.wring = self.wring_t.ap()
        self.wres = [P.res(dma=True, persistent=True) for _ in range(NWB)]
        for r_ in self.wres:
            P.nobarrier.add(r_.sem)
        self.pers_t = nc.alloc_sbuf_tensor("pers", [128, 2048], F32)
        self.pers = self.pers_t.ap()
        self.pers_off = 0
        ARENA_F32 = 37 * 1024
        self.arena_t = nc.alloc_sbuf_tensor("arena", [128, ARENA_F32], F32)
        self.arena = self.arena_t.ap()
        self.arena_cap = ARENA_F32 * 4
        self.arena_off = 0
        self.psum = []
        for i in range(8):
            t = nc.alloc_psum_tensor("ps%d" % i, [128, 512], F32)
            pr_ = P.res()
            pr_.excl = True
            self.psum.append((t.ap(), pr_))
        self.ps_i = 0
        self.ws_descs = []
        self.ws_idx = 0
        self.ws_issued = 0
        self.hold = 2
        self.dumped = set()

    def palloc(self, nf32):
        a = self.pers[:, self.pers_off:self.pers_off + nf32]
        self.pers_off += nf32
        assert self.pers_off <= 2048
        return a

    def areset(self):
        self.P.barrier()
        self.P.release_phase()
        self.arena_off = 0

    def reclaim_u(self):
        self.P.barrier()
        self.arena_off = self.after_u

    def aalloc(self, nelem, dt=F32):
        nbytes = nelem * (4 if dt == F32 else 2)
        nbytes = (nbytes + 63) // 64 * 64
        assert self.arena_off + nbytes <= self.arena_cap, (self.arena_off, nbytes)
        a = self.arena[:, self.arena_off // 4:(self.arena_off + nbytes) // 4]
        self.arena_off += nbytes
        if dt != F32:
            a = a.bitcast(dt)
        return a[:, 0:nelem]

    def ps(self):
        ap, r = self.psum[self.ps_i]
        self.ps_i = (self.ps_i + 1) % 8
        return ap, r

    def wget(self, src, KG, NW):
        i = self.ws_idx
        self.ws_idx += 1
        slot = i % NWB
        view = self.wring[:, slot * WSLOT: slot * WSLOT + KG * NW].rearrange("p (k n) -> p k n", n=NW)
        if self.P.dry:
            self.ws_descs.append((src, KG, NW))
            return view, self.wres[slot]
        while self.ws_issued < min(len(self.ws_descs), i + NWB - self.hold):
            j = self.ws_issued
            s_src, s_kg, s_nw = self.ws_descs[j]
            sl = j % NWB
            dst = self.wring[:, sl * WSLOT: sl * WSLOT + s_kg * s_nw].rearrange("p (k n) -> p k n", n=s_nw)
            srcv = s_src.rearrange("(kc p) n -> p kc n", p=128)
            self.P.dma("pool", (lambda e, d=dst, s=srcv: e.dma_start(out=d, in_=s)),
                       self.wres[sl], reads=(), writes=(self.wres[sl],))
            self.ws_issued += 1
        return view, self.wres[slot]

    def mm(self, ps, psr, lhsT, rhs, start, stop, reads):
        self.P.op("pe", (lambda e: e.matmul(ps, lhsT=lhsT, rhs=rhs, start=start, stop=stop)),
                  reads=reads, writes=(psr,))

    def act(self, out, in_, func, reads, writes, bias=None, scale=None):
        kw = {}
        if bias is not None:
            kw["bias"] = bias
        if scale is not None:
            kw["scale"] = scale
        self.P.op("act", (lambda e: e.activation(out=out, in_=in_, func=func, **kw)), reads=reads, writes=writes)

    def tt(self, out, in0, in1, op, reads, writes, eng="dve"):
        self.P.op(eng, (lambda e: e.tensor_tensor(out=out, in0=in0, in1=in1, op=op)), reads=reads, writes=writes)

    def ts(self, out, in0, s1, op0, reads, writes, s2=None, op1=None, eng="dve"):
        if op1 is None:
            self.P.op(eng, (lambda e: e.tensor_scalar(out=out, in0=in0, scalar1=s1, scalar2=None, op0=op0)),
                      reads=reads, writes=writes)
        else:
            self.P.op(eng, (lambda e: e.tensor_scalar(out=out, in0=in0, scalar1=s1, scalar2=s2, op0=op0, op1=op1)),
                      reads=reads, writes=writes)

    def stt(self, out, in0, scalar, in1, op0, op1, reads, writes, eng="dve"):
        self.P.op(eng, (lambda e: e.scalar_tensor_tensor(out=out, in0=in0, scalar=scalar, in1=in1, op0=op0, op1=op1)),
                  reads=reads, writes=writes)

    def copy(self, out, in_, reads, writes, eng="dve"):
        if eng == "act":
            self.P.op("act", (lambda e: e.activation(out=out, in_=in_, func=AF.Copy)), reads=reads, writes=writes)
        else:
            self.P.op(eng, (lambda e: e.tensor_copy(out=out, in_=in_)), reads=reads, writes=writes)

    def load(self, out, in_, sres, reads=(), q="sp", slow=False):
        if slow:
            fn = (lambda e: e.dma_start(out=out, in_=in_, allow_slow_non_contiguous=True))
        else:
            fn = (lambda e: e.dma_start(out=out, in_=in_))
        self.P.dma(q, fn, sres, reads=reads, writes=(sres,))

    def store(self, out, in_, sres, writes, q="sp"):
        self.P.dma(q, (lambda e: e.dma_start(out=out, in_=in_)), sres, reads=(sres,), writes=writes)

    def dump(self, name, ap, reads):
        if not self.dbg or self.P.dry or name in self.dumped:
            return
        self.dumped.add(name)
        shp = [int(x) for x in ap.shape]
        n = 1
        for x in shp[1:]:
            n *= x
        t = self.nc.dram_tensor("dump_" + name, [shp[0], n], ap.dtype, kind="ExternalOutput").ap()
        r = self.P.res(dma=True, persistent=True)
        src = ap
        if len(shp) == 3:
            t = t.rearrange("p (a b) -> p a b", b=shp[2])
        self.P.dma("sp", (lambda e: e.dma_start(out=t, in_=src)), r, reads=tuple(reads), writes=(r,))

    def snap(self, name):
        if not self.dbg or self.P.dry:
            return
        t = self.nc.dram_tensor("snap_" + name, [D, TT], F32, kind="ExternalOutput").ap()
        r = self.P.res(dma=True, persistent=True)
        self.P.barrier()
        self.P.dma("sp", (lambda e: e.dma_start(out=t, in_=self.XT)), r, reads=tuple(self.xt_res), writes=(r,))
        self.P.barrier()

    def load_vec(self, dst, src_vec, sres):
        self.load(dst, src_vec.rearrange("(k p) -> p k", p=128), sres, slow=True)

    def setup(self):
        P = self.P
        self.const_res = P.res(dma=True, persistent=True)
        self.ident = self.palloc(128)
        self.load(self.ident, self.ident_in, self.const_res)
        self.ones_bf = self.palloc(64).bitcast(BF16)
        self.ones_res = P.res()
        P.op("dve", (lambda e: e.memset(self.ones_bf, 1.0)), writes=(self.ones_res,))
        self.ng = self.palloc(12 * 16)
        self.ng_res = P.res(dma=True, persistent=True)
        for i in range(self.hi - self.lo):
            for j in range(3):
                o = (i * 3 + j) * 16
                self.load_vec(self.ng[:, o:o + 16], self.norm_g[i, j], self.ng_res)
        self.fg = self.palloc(16)
        if self.hi == 4:
            self.load_vec(self.fg, self.final_g, self.ng_res)
        cv = self.palloc(32)
        self.cv_res = P.res(dma=True, persistent=True)
        for r in range(2):
            self.load_vec(cv[:, r * 16:(r + 1) * 16], self.cvec[r], self.cv_res)
        self.sc_bf = self.palloc(16).bitcast(BF16)
        self.sc_res = P.res()
        scv = self.sc_bf.rearrange("p (k r) -> p k r", r=2)
        for r in range(2):
            self.act(scv[:, :, r], cv[:, r * 16:(r + 1) * 16], AF.Silu, reads=(self.cv_res,), writes=(self.sc_res,))
        self.bm = self.palloc(144)
        self.bm_res = P.res(dma=True, persistent=True)
        self.modv = [self.palloc(144), self.palloc(144)]
        self.mod_res = P.res()
        self.der = self.palloc(16 * 8)
        self.der_res = P.res()
        self.vecs = self.palloc(16 * 40)
        self.vec_res = P.res(dma=True, persistent=True)
        self.vec2_res = P.res()

    def transpose_in(self):
        P = self.P
        self.areset()
        tin = [self.aalloc(D), self.aalloc(D)]
        tin_r = [P.res(dma=True), P.res(dma=True)]
        tout = [self.aalloc(D), self.aalloc(D)]
        tout_r = [P.res(dma=True), P.res(dma=True)]
        for blk in range(NBLK):
            b = blk % 2
            src = self.x_win[blk * 128:(blk + 1) * 128, :] if blk < 20 else self.ctx_in[(blk - 20) * 128:(blk - 19) * 128, :]
            self.load(tin[b], src, tin_r[b])
            for q in range(4):
                ps, psr = self.ps()
                for c in range(4):
                    k = q * 4 + c
                    o, i_ = ps[:, c * 128:(c + 1) * 128], tin[b][:, k * 128:(k + 1) * 128]
                    P.op("pe", (lambda e, o=o, i_=i_: e.transpose(o, i_, self.ident)),
                         reads=(tin_r[b], self.const_res), writes=(psr,))
                self.copy(tout[b][:, q * 512:(q + 1) * 512], ps, reads=(psr,), writes=(tout_r[b],),
                          eng=("act" if q % 2 else "dve"))
            ti = min(blk // 4, 5)
            self.store(self.XT[:, blk * 128:(blk + 1) * 128].rearrange("(k p) t -> p k t", p=128),
                       tout[b].rearrange("p (k t) -> p k t", t=128), tout_r[b], writes=(self.xt_res[ti],))

    def mod_phase(self, i):
        P = self.P
        for c0 in range(0, 144, 16):
            self.load(self.bm[:, c0:c0 + 16], self.b_mod[i - self.lo, c0 * 128:(c0 + 16) * 128].rearrange("(k p) -> p k", p=128),
                      self.bm_res, slow=True)
        ps, psr = self.ps()
        sc = self.sc_bf.rearrange("p (k r) -> p k r", r=2)
        self.hold = 0
        for sl in range(72):
            slab, sres = self.wget(self.w_mod[i - self.lo, :, sl * 256:(sl + 1) * 256], 16, 256)
            for cc in range(2):
                c = sl * 2 + cc
                for k in range(KC):
                    self.mm(ps[:, 2 * c:2 * c + 2], psr, slab[:, k, cc * 128:(cc + 1) * 128], sc[:, k, :],
                            k == 0, k == KC - 1, reads=(sres, self.sc_res))
        pv = ps[:, 0:288].rearrange("p (c r) -> p c r", r=2)
        for v in range(2):
            self.tt(self.modv[v], pv[:, :, v], self.bm, ALU.add, reads=(psr, self.bm_res), writes=(self.mod_res,))
        self.dump("modv0", self.modv[0], (self.mod_res,))
        self.dump("modv1", self.modv[1], (self.mod_res,))
        self.dump("bm", self.bm, (self.bm_res,))
        self.dump("scbf", self.sc_bf, (self.sc_res,))
        self.dump("ng", self.ng, (self.ng_res,))

    def mslot(self, v, s):
        return self.modv[v][:, s * 16:(s + 1) * 16]

    def derive(self, i, gidx, s_shift, s_scale, s_gate, half):
        o = []
        il = i - self.lo
        g = self.ng[:, (il * 3 + gidx) * 16:(il * 3 + gidx) * 16 + 16]
        for v in range(2):
            A = self.der[:, (v * 4) * 16:(v * 4 + 1) * 16]
            G = self.der[:, (v * 4 + 1) * 16:(v * 4 + 2) * 16]
            self.stt(A, self.mslot(v, s_scale), 1.0, g, ALU.add, ALU.mult,
                     reads=(self.mod_res, self.ng_res), writes=(self.der_res,))
            self.ts(G, self.mslot(v, s_gate), 0.5 if half else 1.0, ALU.mult, reads=(self.mod_res,), writes=(self.der_res,))
            o.append((A, self.mslot(v, s_shift), G))
        return o

    def norm_setup(self):
        P = self.P
        self.sq = [self.aalloc(512, BF16), self.aalloc(512, BF16)]
        self.sq_r = [P.res(), P.res()]
        self.srt = self.aalloc(512)
        self.srt_r = P.res()
        self.rstd = self.aalloc(512)
        self.rstd_r = P.res()
        self.tmpn = [self.aalloc(512), self.aalloc(512)]
        self.tmpn_r = [P.res(), P.res()]

    def rms_stats(self, xs, xs_r, n):
        ps, psr = self.ps()
        for k in range(KC):
            b = k % 2
            self.act(self.sq[b][:, :n], xs[:, k, :n], AF.Square, reads=(xs_r,), writes=(self.sq_r[b],))
            self.mm(ps[:, :n], psr, self.ones_bf, self.sq[b][:, :n], k == 0, k == KC - 1,
                    reads=(self.sq_r[b], self.ones_res))
        self.act(self.srt[:, :n], ps[:, :n], AF.Sqrt, reads=(psr,), writes=(self.srt_r,), bias=self.eps_ap, scale=1.0 / D)
        self.P.op("dve", (lambda e, o=self.rstd[:, :n], i_=self.srt[:, :n]: e.reciprocal(out=o, in_=i_)),
                  reads=(self.srt_r,), writes=(self.rstd_r,))

    def norm_tile(self, tile_i, xs, xs_r, A, B, udst, udst_r):
        t0, n, v = TILES[tile_i]
        self.load(xs[:, :, :n], self.XT[:, t0:t0 + n].rearrange("(k p) t -> p k t", p=128), xs_r,
                  reads=(self.xt_res[tile_i],))
        self.rms_stats(xs, xs_r, n)
        for k in range(KC):
            b = k % 2
            self.tt(self.tmpn[b][:, :n], xs[:, k, :n], self.rstd[:, :n], ALU.mult,
                    reads=(xs_r, self.rstd_r), writes=(self.tmpn_r[b],))
            self.act(udst[:, k, :n], self.tmpn[b][:, :n], AF.Identity, reads=(self.tmpn_r[b], self.der_res, self.mod_res),
                     writes=(udst_r,), bias=B[:, k:k + 1], scale=A[:, k:k + 1])

    def out_proj(self, W, KG, rhs_fn, groups, gates, bias=None):
        P = self.P
        for grp in groups:
            rv, rres = rhs_fn(grp)
            self.hold = 0
            for m in range(KC):
                slab, sres = self.wget(W[:, m * 128:(m + 1) * 128], KG, 128)
                for s, ti in enumerate(grp):
                    t0, n, v = TILES[ti]
                    b = self.xr_i
                    self.xr_i = (self.xr_i + 1) % 4
                    xr, xr_r = self.xres[b], self.xres_r[b]
                    self.load(xr[:, :n], self.XT[m * 128:(m + 1) * 128, t0:t0 + n], xr_r, reads=(self.xt_res[ti],))
                    ps, psr = self.ps()
                    for j in range(KG):
                        self.mm(ps[:, :n], psr, slab[:, j, :], rv(s, j, n), j == 0, j == KG - 1, reads=(sres,) + tuple(rres))
                    G = gates[v]
                    if bias is not None:
                        self.stt(xr[:, :n], ps[:, :n], G[:, m:m + 1], xr[:, :n], ALU.mult, ALU.add,
                                 reads=(psr, self.der_res, self.mod_res), writes=(xr_r,))
                        self.ts(xr[:, :n], xr[:, :n], bias[v][:, m:m + 1], ALU.add, reads=(self.vec2_res,), writes=(xr_r,))
                    else:
                        self.stt(xr[:, :n], ps[:, :n], G[:, m:m + 1], xr[:, :n], ALU.mult, ALU.add,
                                 reads=(psr, self.der_res, self.mod_res), writes=(xr_r,))
                    self.store(self.XT[m * 128:(m + 1) * 128, t0:t0 + n], xr[:, :n], xr_r, writes=(self.xt_res[ti],))

    def xres_setup(self):
        self.xres = [self.aalloc(512) for _ in range(4)]
        self.xres_r = [self.P.res(dma=True) for _ in range(4)]
        self.xr_i = 0

    def ffn_phase(self, i, which, with_ctx):
        P = self.P
        self.areset()
        uT = self.aalloc(KC * 1024, BF16).rearrange("p (k t) -> p k t", t=1024)
        u_r = [P.res(), P.res()]
        actT = self.aalloc(FC * 1024, BF16)
        act_r = [P.res(dma=True), P.res(dma=True)]
        act_v = [actT[:, s * FC * 512:(s + 1) * FC * 512].rearrange("p (j t) -> p j t", t=512) for s in range(2)]
        xs_v = [actT[:, s * FC * 512: s * FC * 512 + KC * 512 * 2].bitcast(F32).rearrange("p (k t) -> p k t", t=512)
                for s in range(2)]
        self.norm_setup()
        self.xres_setup()
        sg = [self.aalloc(512), self.aalloc(512)]
        sg_r = [P.res(), P.res()]
        sgi = 0
        Wi = self.ffn_w_in[i - self.lo, which]
        Wo = self.ffn_w_out[i - self.lo, which]
        if which == 0:
            dv = self.derive(i, 0, 0, 1, 2, True)
        else:
            dv = self.derive(i, 2, 6, 7, 8, True)
        groups = [[0, 1], [2, 3], [4, 5]] if with_ctx else [[0, 1], [2, 3]]
        gates = [dv[0][2], dv[1][2]]

        def rhs_fn(grp):
            nonlocal sgi
            for s, ti in enumerate(grp):
                t0, n, v = TILES[ti]
                A, B, G = dv[v]
                self.norm_tile(ti, xs_v[s], act_r[s], A, B, uT[:, :, s * 512:s * 512 + n], u_r[s])
                self.dump("rstd%d" % s, self.rstd, (self.rstd_r,))
                self.dump("srt%d" % s, self.srt, (self.srt_r,))
                self.dump("xs%d" % s, xs_v[s], (act_r[s],))
            self.dump("uT", uT, (u_r[0], u_r[1]))
            self.dump("der", self.der, (self.der_res,))
            self.hold = 1
            for jp in range(FC // 2):
                slabG, rG = self.wget(Wi[:, jp * 256:(jp + 1) * 256], KC, 256)
                slabU, rU = self.wget(Wi[:, FF + jp * 256:FF + (jp + 1) * 256], KC, 256)
                for jj in range(2):
                    j = jp * 2 + jj
                    for s, ti in enumerate(grp):
                        t0, n, v = TILES[ti]
                        psg, psg_r = self.ps()
                        psu, psu_r = self.ps()
                        for k in range(KC):
                            self.mm(psg[:, :n], psg_r, slabG[:, k, jj * 128:(jj + 1) * 128], uT[:, k, s * 512:s * 512 + n],
                                    k == 0, k == KC - 1, reads=(rG, u_r[s]))
                        for k in range(KC):
                            self.mm(psu[:, :n], psu_r, slabU[:, k, jj * 128:(jj + 1) * 128], uT[:, k, s * 512:s * 512 + n],
                                    k == 0, k == KC - 1, reads=(rU, u_r[s]))
                        b = sgi
                        sgi = (sgi + 1) % 2
                        self.act(sg[b][:, :n], psg[:, :n], AF.Silu, reads=(psg_r,), writes=(sg_r[b],))
                        self.tt(act_v[s][:, j, :n], sg[b][:, :n], psu[:, :n], ALU.mult,
                                reads=(sg_r[b], psu_r), writes=(act_r[s],))
                        if j == 1:
                            self.dump("act%d" % s, act_v[s][:, 0:2, :], (act_r[s],))
            return (lambda s, j, n: act_v[s][:, j, :n]), (act_r[0], act_r[1])

        self.out_proj(Wo, FC, rhs_fn, groups, gates)

    def final_phase(self):
        P = self.P
        self.areset()
        self.norm_setup()
        xs = [self.aalloc(KC * 128).rearrange("p (k t) -> p k t", t=128) for _ in range(2)]
        xs_r = [P.res(dma=True), P.res(dma=True)]
        yb = [self.aalloc(128), self.aalloc(128)]
        yb_r = [P.res(), P.res()]
        tout = [self.aalloc(D), self.aalloc(D)]
        tout_r = [P.res(dma=True), P.res(dma=True)]
        for blk in range(16):
            b = blk % 2
            ti = blk // 4
            self.load(xs[b], self.XT[:, blk * 128:(blk + 1) * 128].rearrange("(k p) t -> p k t", p=128), xs_r[b],
                      reads=(self.xt_res[ti],))
            self.rms_stats(xs[b], xs_r[b], 128)
            for q in range(4):
                ps, psr = self.ps()
                for c in range(4):
                    k = q * 4 + c
                    yy = yb[k % 2]
                    self.stt(yy, xs[b][:, k, :], self.fg[:, k:k + 1], self.rstd[:, :128], ALU.mult, ALU.mult,
                             reads=(xs_r[b], self.rstd_r, self.ng_res), writes=(yb_r[k % 2],))
                    o = ps[:, c * 128:(c + 1) * 128]
                    P.op("pe", (lambda e, o=o, yy=yy: e.transpose(o, yy, self.ident)),
                         reads=(yb_r[k % 2], self.const_res), writes=(psr,))
                self.copy(tout[b][:, q * 512:(q + 1) * 512], ps, reads=(psr,), writes=(tout_r[b],),
                          eng=("act" if q % 2 else "dve"))
            self.store(self.out[blk * 128:(blk + 1) * 128, :], tout[b], tout_r[b], writes=(self.out_res,))

    def mixer_u(self, i, tiles=range(6)):
        P = self.P
        self.uall = self.aalloc(KC * TT, BF16).rearrange("p (k t) -> p k t", t=TT)
        self.uall_r = P.res()
        self.after_u = self.arena_off
        xs = self.aalloc(KC * 512).rearrange("p (k t) -> p k t", t=512)
        xs_r = P.res(dma=True)
        self.norm_setup()
        dv = self.derive(i, 1, 3, 4, 5, False)
        for ti in tiles:
            t0, n, v = TILES[ti]
            A, B, G = dv[v]
            self.norm_tile(ti, xs, xs_r, A, B, self.uall[:, :, t0:t0 + n], self.uall_r)
        self.reclaim_u()
        return [dv[0][2], dv[1][2]]

    def rhs_from_dram(self, SRC, src_res, rb, rb_r):
        def rhs_fn(grp):
            for s, ti in enumerate(grp):
                t0, n, v = TILES[ti]
                self.load(rb[s][:, :, :n], SRC[:, t0:t0 + n].rearrange("(k p) t -> p k t", p=128), rb_r[s],
                          reads=tuple(src_res))
            return (lambda s, j, n: rb[s][:, j, :n]), (rb_r[0], rb_r[1])
        return rhs_fn

    def mixer_sconv(self, i):
        P = self.P
        self.areset()
        gates = self.mixer_u(i)
        wc = self.vecs[:, 0:48]
        for k in range(3):
            self.load_vec(wc[:, k * 16:(k + 1) * 16], self.sc_w_conv[k], self.vec_res)
        vb = self.aalloc(WIN + 2)
        vc = self.aalloc(NCTX + 2)
        v_r = P.res()
        bgb = self.aalloc(TT)
        bg_r = P.res()
        acc = self.aalloc(TT)
        acc_r = P.res()
        tmpc = [self.aalloc(512), self.aalloc(512)]
        tmpc_r = [P.res(), P.res()]
        sst = [self.aalloc(TT, BF16), self.aalloc(TT, BF16)]
        sst_r = [P.res(dma=True), P.res(dma=True)]
        P.op("dve", (lambda e: e.memset(vb, 0.0)), writes=(v_r,))
        P.op("dve", (lambda e: e.memset(vc, 0.0)), writes=(v_r,))
        W = self.sc_w_in
        ci = 0
        self.hold = 2
        for m in range(KC):
            sb_, rb_ = self.wget(W[:, m * 128:(m + 1) * 128], KC, 128)
            sc_, rc_ = self.wget(W[:, D + m * 128:D + (m + 1) * 128], KC, 128)
            sx_, rx_ = self.wget(W[:, 2 * D + m * 128:2 * D + (m + 1) * 128], KC, 128)
            for ti, (t0, n, v) in enumerate(TILES):
                pb, pb_r = self.ps()
                pc, pc_r = self.ps()
                px, px_r = self.ps()
                for (pp, pr, sl, rr) in ((pb, pb_r, sb_, rb_), (pc, pc_r, sc_, rc_), (px, px_r, sx_, rx_)):
                    for k in range(KC):
                        self.mm(pp[:, :n], pr, sl[:, k, :], self.uall[:, k, t0:t0 + n], k == 0, k == KC - 1,
                                reads=(rr, self.uall_r))
                b = ci % 2
                ci += 1
                self.copy(tmpc[b][:, :n], pc[:, :n], reads=(pc_r,), writes=(tmpc_r[b],), eng="act")
                dstv = vb[:, 1 + t0:1 + t0 + n] if v == 0 else vc[:, 1:1 + n]
                self.tt(dstv, tmpc[b][:, :n], px[:, :n], ALU.mult, reads=(tmpc_r[b], px_r), writes=(v_r,))
                self.copy(bgb[:, t0:t0 + n], pb[:, :n], reads=(pb_r,), writes=(bg_r,), eng="act")
            for (src, N, o0) in ((vb, 2048 + TILES[4][1], 0), (vc, NCTX, WIN)):
                a = acc[:, o0:o0 + N]
                self.ts(a, src[:, 0:N], wc[:, m:m + 1], ALU.mult, reads=(v_r, self.vec_res), writes=(acc_r,))
                self.stt(a, src[:, 1:1 + N], wc[:, 16 + m:17 + m], a, ALU.mult, ALU.add, reads=(v_r, self.vec_res), writes=(acc_r,))
                self.stt(a, src[:, 2:2 + N], wc[:, 32 + m:33 + m], a, ALU.mult, ALU.add, reads=(v_r, self.vec_res), writes=(acc_r,))
            b = m % 2
            self.tt(sst[b], acc, bgb, ALU.mult, reads=(acc_r, bg_r), writes=(sst_r[b],))
            self.store(self.ST[m * 128:(m + 1) * 128, :], sst[b], sst_r[b], writes=(self.st_res[m],))
        self.areset()
        self.xres_setup()
        rb = [self.aalloc(KC * 512, BF16).rearrange("p (k t) -> p k t", t=512) for _ in range(2)]
        rb_r = [P.res(dma=True), P.res(dma=True)]
        self.out_proj(self.sc_w_out, KC, self.rhs_from_dram(self.ST, self.st_res, rb, rb_r),
                      [[0, 1], [2, 3], [4, 5]], gates)

    def mixer_conformer(self, i):
        P = self.P
        self.areset()
        gates = self.mixer_u(i)
        V = self.vecs
        wdw = V[:, 0:31 * 16]
        for k in range(31):
            self.load_vec(wdw[:, k * 16:(k + 1) * 16], self.cv_w_dw[k], self.vec_res)
        o = 31 * 16
        b1 = V[:, o:o + 32]
        self.load_vec(b1, self.cv_b_pw1, self.vec_res)
        bdw = V[:, o + 32:o + 48]
        self.load_vec(bdw, self.cv_b_dw, self.vec_res)
        lng = V[:, o + 48:o + 64]
        self.load_vec(lng, self.cv_ln_g, self.vec_res)
        lnb = V[:, o + 64:o + 80]
        self.load_vec(lnb, self.cv_ln_b, self.vec_res)
        b2 = V[:, o + 80:o + 96]
        self.load_vec(b2, self.cv_b_pw2, self.vec_res)
        b2g = [V[:, o + 96:o + 112], V[:, o + 112:o + 128]]
        for v in range(2):
            self.tt(b2g[v], b2, gates[v], ALU.mult, reads=(self.vec_res, self.der_res, self.mod_res), writes=(self.vec2_res,))
        zb = self.aalloc(WIN + 30)
        zc = self.aalloc(NCTX + 30)
        z_r = P.res()
        acc = [self.aalloc(TT), self.aalloc(TT)]
        acc_r = [P.res(dma=True), P.res(dma=True)]
        sgm = [self.aalloc(512), self.aalloc(512)]
        sgm_r = [P.res(), P.res()]
        P.op("dve", (lambda e: e.memset(zb, 0.0)), writes=(z_r,))
        P.op("dve", (lambda e: e.memset(zc, 0.0)), writes=(z_r,))
        W = self.cv_w_pw1
        ci = 0
        self.hold = 1
        for m in range(KC):
            sa_, ra_ = self.wget(W[:, m * 128:(m + 1) * 128], KC, 128)
            sg_, rg_ = self.wget(W[:, D + m * 128:D + (m + 1) * 128], KC, 128)
            for ti, (t0, n, v) in enumerate(TILES):
                pa, pa_r = self.ps()
                pg, pg_r = self.ps()
                for (pp, pr, sl, rr) in ((pa, pa_r, sa_, ra_), (pg, pg_r, sg_, rg_)):
                    for k in range(KC):
                        self.mm(pp[:, :n], pr, sl[:, k, :], self.uall[:, k, t0:t0 + n], k == 0, k == KC - 1,
                                reads=(rr, self.uall_r))
                b = ci % 2
                ci += 1
                self.act(sgm[b][:, :n], pg[:, :n], AF.Sigmoid, reads=(pg_r, self.vec_res), writes=(sgm_r[b],),
                         bias=b1[:, 16 + m:17 + m])
                dstv = zb[:, 15 + t0:15 + t0 + n] if v == 0 else zc[:, 15:15 + n]
                self.stt(dstv, pa[:, :n], b1[:, m:m + 1], sgm[b][:, :n], ALU.add, ALU.mult,
                         reads=(pa_r, sgm_r[b], self.vec_res), writes=(z_r,))
            ab = m % 2
            for (src, N, o0) in ((zb, 2048 + TILES[4][1], 0), (zc, NCTX, WIN)):
                a = acc[ab][:, o0:o0 + N]
                self.ts(a, src[:, 0:N], wdw[:, m:m + 1], ALU.mult, reads=(z_r, self.vec_res), writes=(acc_r[ab],),
                        s2=bdw[:, m:m + 1], op1=ALU.add)
                for k in range(1, 31):
                    self.stt(a, src[:, k:k + N], wdw[:, k * 16 + m:k * 16 + m + 1], a, ALU.mult, ALU.add,
                             reads=(z_r, self.vec_res), writes=(acc_r[ab],))
            self.store(self.CT[m * 128:(m + 1) * 128, :], acc[ab], acc_r[ab], writes=(self.ct_res[m],))
        self.areset()
        self.xres_setup()
        self.norm_setup()
        rb = [self.aalloc(KC * 512, BF16).rearrange("p (k t) -> p k t", t=512) for _ in range(2)]
        rb_r = [P.res(), P.res()]
        cs = self.aalloc(KC * 512).rearrange("p (k t) -> p k t", t=512)
        cs_r = P.res(dma=True)
        cb16 = [self.aalloc(512, BF16), self.aalloc(512, BF16)]
        cb16_r = [P.res(), P.res()]
        mean = self.aalloc(512)
        mean_r = P.res()
        msq = self.aalloc(512)
        msq_r = P.res()

        def rhs_fn(grp):
            for s, ti in enumerate(grp):
                t0, n, v = TILES[ti]
                self.load(cs[:, :, :n], self.CT[:, t0:t0 + n].rearrange("(k p) t -> p k t", p=128), cs_r,
                          reads=tuple(self.ct_res))
                p1, p1_r = self.ps()
                p2, p2_r = self.ps()
                for k in range(KC):
                    b = k % 2
                    self.copy(cb16[b][:, :n], cs[:, k, :n], reads=(cs_r,), writes=(cb16_r[b],), eng="dve")
                    self.mm(p1[:, :n], p1_r, self.ones_bf, cb16[b][:, :n], k == 0, k == KC - 1, reads=(cb16_r[b], self.ones_res))
                    self.act(self.sq[b][:, :n], cs[:, k, :n], AF.Square, reads=(cs_r,), writes=(self.sq_r[b],))
                    self.mm(p2[:, :n], p2_r, self.ones_bf, self.sq[b][:, :n], k == 0, k == KC - 1, reads=(self.sq_r[b], self.ones_res))
                self.ts(mean[:, :n], p1[:, :n], 1.0 / D, ALU.mult, reads=(p1_r,), writes=(mean_r,))
                self.tt(msq[:, :n], mean[:, :n], mean[:, :n], ALU.mult, reads=(mean_r,), writes=(msq_r,))
                self.stt(msq[:, :n], p2[:, :n], 1.0 / D, msq[:, :n], ALU.mult, ALU.subtract, reads=(p2_r,), writes=(msq_r,))
                self.ts(msq[:, :n], msq[:, :n], 0.0, ALU.max, reads=(), writes=(msq_r,))
                self.act(self.srt[:, :n], msq[:, :n], AF.Sqrt, reads=(msq_r,), writes=(self.srt_r,), bias=self.eps_ap)
                P.op("dve", (lambda e, o=self.rstd[:, :n], i_=self.srt[:, :n]: e.reciprocal(out=o, in_=i_)),
                     reads=(self.srt_r,), writes=(self.rstd_r,))
                for k in range(KC):
                    b = k % 2
                    self.tt(self.tmpn[b][:, :n], cs[:, k, :n], mean[:, :n], ALU.subtract, reads=(cs_r, mean_r), writes=(self.tmpn_r[b],))
                    self.tt(self.tmpn[b][:, :n], self.tmpn[b][:, :n], self.rstd[:, :n], ALU.mult, reads=(self.rstd_r,), writes=(self.tmpn_r[b],))
                    self.act(rb[s][:, k, :n], self.tmpn[b][:, :n], AF.Silu, reads=(self.tmpn_r[b], self.vec_res), writes=(rb_r[s],),
                             bias=lnb[:, k:k + 1], scale=lng[:, k:k + 1])
            return (lambda s, j, n: rb[s][:, j, :n]), (rb_r[0], rb_r[1])

        self.out_proj(self.cv_w_pw2, KC, rhs_fn, [[0, 1], [2, 3], [4, 5]], gates, bias=b2g)

    def run_units(self, units):
        n = len(units)
        for k in range(n + 1):
            if k < n:
                pre, A, B, post = units[k]
                if pre is not None:
                    pre()
                A(k % 2)
            if k >= 1:
                pre, A, B, post = units[k - 1]
                B((k - 1) % 2)
                if post is not None:
                    post()

    def proj_T(self, W, col0, nchunks, DST, dst_res, st, st_r, evac):
        self.hold = 0
        for m in range(nchunks):
            slab, sres = self.wget(W[:, col0 + m * 128:col0 + (m + 1) * 128], KC, 128)
            b = m % 2
            for ti, (t0, n, v) in enumerate(TILES):
                if n == 0:
                    continue
                ps, psr = self.ps()
                for k in range(KC):
                    self.mm(ps[:, :n], psr, slab[:, k, :], self.uall[:, k, t0:t0 + n], k == 0, k == KC - 1,
                            reads=(sres, self.uall_r))
                evac(m, ti, ps, psr, st[b][:, t0:t0 + n], st_r[b])
            self.store(DST[m * 128:(m + 1) * 128, :], st[b], st_r[b], writes=(dst_res[m],))

    def proj_V(self, W, col0, ncols, vst, vst_r, blocks=None):
        i = 0
        if blocks is None:
            blocks = list(range(NBLK))
        self.hold = 0
        for c0 in range(0, ncols, 256):
            slab, sres = self.wget(W[:, col0 + c0:col0 + c0 + 256], KC, 256)
            for blk in blocks:
                ps, psr = self.ps()
                for k in range(KC):
                    self.mm(ps[:, :256], psr, self.uall[:, k, blk * 128:(blk + 1) * 128], slab[:, k, :], k == 0, k == KC - 1,
                            reads=(sres, self.uall_r))
                b = i % 2
                i += 1
                self.copy(vst[b], ps[:, :256], reads=(psr,), writes=(vst_r[b],), eng=("act" if b else "dve"))
                self.store(self.VV[blk * 128:(blk + 1) * 128, c0:c0 + 256], vst[b], vst_r[b], writes=(self.vv_res[blk],))

    def mixer_na(self, i):
        P = self.P
        self.areset()
        gates = self.mixer_u(i)
        st = [self.aalloc(TT, BF16), self.aalloc(TT, BF16)]
        st_r = [P.res(dma=True), P.res(dma=True)]
        vst = [self.aalloc(256, BF16), self.aalloc(256, BF16)]
        vst_r = [P.res(dma=True), P.res(dma=True)]
        cnt = [0]

        def evac(m, ti, ps, psr, dst, dst_r):
            n = TILES[ti][1]
            cnt[0] += 1
            self.copy(dst, ps[:, :n], reads=(psr,), writes=(dst_r,), eng=("act" if cnt[0] % 2 else "dve"))

        W = self.na_w_qkv
        self.proj_T(W, 0, 16, self.QT, self.qt_res, st, st_r, evac)
        self.proj_T(W, D, 16, self.KT, self.kt_res, st, st_r, evac)
        self.proj_V(W, 2 * D, D, vst, vst_r)
        self.areset()
        qh = [self.aalloc(TT, BF16) for _ in range(2)]
        kh = [self.aalloc(TT, BF16) for _ in range(2)]
        vh = [self.aalloc(NBLK * 128, BF16).rearrange("p (b d) -> p b d", d=128) for _ in range(2)]
        bia = [self.aalloc(25 * 128) for _ in range(2)]
        hr = [[P.res(dma=True) for _ in range(4)] for _ in range(2)]
        ost = [self.aalloc(TT, BF16) for _ in range(2)]
        ost_r = [P.res(dma=True), P.res(dma=True)]
        sb = [self.aalloc(640) for _ in range(2)]
        sb_r = [P.res(), P.res()]
        pT = [self.aalloc(896, BF16) for _ in range(2)]
        pT_r = [P.res(), P.res()]
        rec = [self.aalloc(256) for _ in range(2)]
        rec_r = [P.res(), P.res()]
        lnb = [self.aalloc(256) for _ in range(2)]
        lnb_r = [P.res(), P.res()]

        def na_unit(h, hb, p):
            bs = min(max(2 * p - 4, 0), 30)
            var = {0: 0, 1: 1, 18: 3, 19: 4}.get(p, 2)
            q0 = p * 128
            ktoks = [bs * 64 + c * 128 for c in range(5)] + [WIN, WIN + 128]
            vblk = [t // 128 for t in ktoks]

            def A(u):
                ps1, ps1_r = self.ps()
                ps2, ps2_r = self.ps()
                for c in range(7):
                    pp, pr = (ps1, ps1_r) if c < 4 else (ps2, ps2_r)
                    cc = c % 4
                    self.mm(pp[:, cc * 128:(cc + 1) * 128], pr, kh[hb][:, ktoks[c]:ktoks[c] + 128], qh[hb][:, q0:q0 + 128],
                            True, True, reads=(hr[hb][0], hr[hb][1]))
                bv = bia[hb][:, var * 640:(var + 1) * 640]
                self.stt(sb[u][:, 0:512], ps1, SCALE, bv[:, 0:512], ALU.mult, ALU.add,
                         reads=(ps1_r, hr[hb][3]), writes=(sb_r[u],))
                self.stt(sb[u][:, 512:640], ps2[:, 0:128], SCALE, bv[:, 512:640], ALU.mult, ALU.add,
                         reads=(ps2_r, hr[hb][3]), writes=(sb_r[u],))
                self.act(pT[u][:, 0:640], sb[u], AF.Exp, reads=(sb_r[u],), writes=(pT_r[u],))
                self.act(pT[u][:, 640:896], ps2[:, 128:384], AF.Exp, reads=(ps2_r,), writes=(pT_r[u],), scale=SCALE)

            def B(u):
                po, po_r = self.ps()
                pz, pz_r = self.ps()
                for c in range(7):
                    self.mm(po[:, 0:128], po_r, vh[hb][:, vblk[c], :], pT[u][:, c * 128:(c + 1) * 128], c == 0, c == 6,
                            reads=(hr[hb][2], pT_r[u]))
                for c in range(7):
                    self.mm(pz[:, 0:128], pz_r, self.ones_bf, pT[u][:, c * 128:(c + 1) * 128], c == 0, c == 6,
                            reads=(self.ones_res, pT_r[u]))
                self.act(lnb[u][:, 0:128], pz[:, 0:128], AF.Ln, reads=(pz_r,), writes=(lnb_r[u],))
                self.act(rec[u][:, 0:128], lnb[u][:, 0:128], AF.Exp, reads=(lnb_r[u],), writes=(rec_r[u],), scale=-1.0)
                self.tt(ost[hb][:, q0:q0 + 128], po[:, 0:128], rec[u][:, 0:128], ALU.mult,
                        reads=(po_r, rec_r[u]), writes=(ost_r[hb],))
            return A, B

        def na_ctx_unit(h, hb):
            def A(u):
                ps1, ps1_r = self.ps()
                for c in range(2):
                    self.mm(ps1[:, c * 256:(c + 1) * 256], ps1_r, kh[hb][:, WIN + c * 128:WIN + (c + 1) * 128], qh[hb][:, WIN:TT],
                            True, True, reads=(hr[hb][0], hr[hb][1]))
                self.act(pT[u][:, 0:512], ps1, AF.Exp, reads=(ps1_r,), writes=(pT_r[u],), scale=SCALE)

            def B(u):
                po, po_r = self.ps()
                pz, pz_r = self.ps()
                for c in range(2):
                    self.mm(po[:, 0:256], po_r, vh[hb][:, 20 + c, :], pT[u][:, c * 256:(c + 1) * 256], c == 0, c == 1,
                            reads=(hr[hb][2], pT_r[u]))
                for c in range(2):
                    self.mm(pz[:, 0:256], pz_r, self.ones_bf, pT[u][:, c * 256:(c + 1) * 256], c == 0, c == 1,
                            reads=(self.ones_res, pT_r[u]))
                self.act(lnb[u], pz[:, 0:256], AF.Ln, reads=(pz_r,), writes=(lnb_r[u],))
                self.act(rec[u], lnb[u], AF.Exp, reads=(lnb_r[u],), writes=(rec_r[u],), scale=-1.0)
                self.tt(ost[hb][:, WIN:TT], po[:, 0:256], rec[u], ALU.mult, reads=(po_r, rec_r[u]), writes=(ost_r[hb],))
            return A, B

        def na_pre(h, hb):
            def f():
                self.load(qh[hb], self.QT[h * 128:(h + 1) * 128, :], hr[hb][0], reads=(self.qt_res[h],))
                self.load(kh[hb], self.KT[h * 128:(h + 1) * 128, :], hr[hb][1], reads=(self.kt_res[h],))
                self.load(vh[hb], self.VV[:, h * 128:(h + 1) * 128].rearrange("(b p) d -> p b d", p=128), hr[hb][2],
                          reads=tuple(self.vv_res))
                self.load(bia[hb], self.na_bias[h], hr[hb][3])
            return f

        def na_post(h, hb):
            def f():
                self.store(self.ST[h * 128:(h + 1) * 128, :], ost[hb], ost_r[hb], writes=(self.st_res[h],))
            return f

        units = []
        for h in range(16):
            hb = h % 2
            for p in range(18):
                A, B = na_unit(h, hb, p)
                units.append((na_pre(h, hb) if p == 0 else None, A, B, None))
            A, B = na_ctx_unit(h, hb)
            units.append((None, A, B, na_post(h, hb)))
        self.run_units(units)
        TILES[4] = (2048, 192, 0)
        self.areset()
        self.xres_setup()
        rb = [self.aalloc(KC * 512, BF16).rearrange("p (k t) -> p k t", t=512) for _ in range(2)]
        rb_r = [P.res(dma=True), P.res(dma=True)]
        self.out_proj(self.na_w_o, KC, self.rhs_from_dram(self.ST, self.st_res, rb, rb_r), [[0, 1], [2, 3], [4, 5]], gates)

    def mixer_swa(self, i):
        P = self.P
        self.areset()
        gates = self.mixer_u(i)
        st = [self.aalloc(TT, BF16), self.aalloc(TT, BF16)]
        st_r = [P.res(dma=True), P.res(dma=True)]
        vst = [self.aalloc(256, BF16), self.aalloc(256, BF16)]
        vst_r = [P.res(dma=True), P.res(dma=True)]
        cosT = self.aalloc(WIN)
        sinT = self.aalloc(WIN)
        tab_r = P.res(dma=True)
        self.load(cosT, self.rope_cos, tab_r)
        self.load(sinT, self.rope_sin, tab_r)
        t1 = [self.aalloc(512), self.aalloc(512)]
        t1_r = [P.res(), P.res()]
        t2 = [self.aalloc(512), self.aalloc(512)]
        t2_r = [P.res(), P.res()]
        cnt = [0]
        W = self.swa_w_qkv
        Wsw = self.swa_w_sw

        def rope_proj(col0, col0sw, nchunks, DST, dst_res, tiles):
            self.hold = 1
            for m in range(nchunks):
                slab, sres = self.wget(W[:, col0 + m * 128:col0 + (m + 1) * 128], KC, 128)
                slab2, sres2 = self.wget(Wsw[:, col0sw + m * 128:col0sw + (m + 1) * 128], KC, 128)
                b = m % 2
                for ti in tiles:
                    t0, n, v = TILES[ti]
                    ps, psr = self.ps()
                    for k in range(KC):
                        self.mm(ps[:, :n], psr, slab[:, k, :], self.uall[:, k, t0:t0 + n], k == 0, k == KC - 1,
                                reads=(sres, self.uall_r))
                    if v == 1:
                        self.copy(st[b][:, t0:t0 + n], ps[:, :n], reads=(psr,), writes=(st_r[b],), eng="act")
                        continue
                    ps2, ps2r = self.ps()
                    for k in range(KC):
                        self.mm(ps2[:, :n], ps2r, slab2[:, k, :], self.uall[:, k, t0:t0 + n], k == 0, k == KC - 1,
                                reads=(sres2, self.uall_r))
                    c = cnt[0] % 2
                    cnt[0] += 1
                    self.tt(t1[c][:, :n], ps[:, :n], cosT[:, t0:t0 + n], ALU.mult, reads=(psr, tab_r), writes=(t1_r[c],))
                    self.tt(t2[c][:, :n], ps2[:, :n], sinT[:, t0:t0 + n], ALU.mult, reads=(ps2r, tab_r), writes=(t2_r[c],))
                    self.tt(st[b][:, t0:t0 + n], t1[c][:, :n], t2[c][:, :n], ALU.add, reads=(t1_r[c], t2_r[c]), writes=(st_r[b],))
                self.store(DST[m * 128:(m + 1) * 128, :], st[b], st_r[b], writes=(dst_res[m],))

        sstop = getattr(self, "swa_stop", 9)
        if sstop <= 1:
            return
        rope_proj(0, 0, 16, self.QT, self.qt_res, (0, 1, 2, 3))
        if sstop <= 2:
            return
        rope_proj(D, D, 4, self.KT, self.kt_res, (0, 1, 2, 3, 4, 5))
        self.proj_V(W, D + 512, 512, vst, vst_r, blocks=list(range(17)) + [20, 21])
        if sstop <= 3:
            return
        self.areset()
        qh = [self.aalloc(TT, BF16) for _ in range(2)]
        kh = [self.aalloc(TT, BF16) for _ in range(2)]
        vh = [self.aalloc(NBLK * 128, BF16).rearrange("p (b d) -> p b d", d=128) for _ in range(2)]
        q_r = [P.res(dma=True), P.res(dma=True)]
        kv_r = [[P.res(dma=True), P.res(dma=True)] for _ in range(2)]
        msk = self.aalloc(256)
        msk_r = P.res(dma=True)
        self.load(msk, self.swa_mask, msk_r)
        esk = self.aalloc(16)
        esk_r = P.res(dma=True)
        esk2_r = P.res()
        self.load(esk, self.swa_sink, esk_r)
        self.act(esk, esk, AF.Exp, reads=(esk_r,), writes=(esk2_r,))
        ost = [self.aalloc(WIN, BF16) for _ in range(2)]
        ost_r = [P.res(dma=True), P.res(dma=True)]
        sb = [self.aalloc(256) for _ in range(2)]
        sb_r = [P.res(), P.res()]
        pT = [self.aalloc(640, BF16) for _ in range(2)]
        pT_r = [P.res(), P.res()]
        rec = [self.aalloc(128) for _ in range(2)]
        rec_r = [P.res(), P.res()]
        den = [self.aalloc(128) for _ in range(2)]
        den_r = [P.res(), P.res()]
        lnb = [self.aalloc(128) for _ in range(2)]
        lnb_r = [P.res(), P.res()]

        def swa_unit(h, hb, kb, qb):
            q0 = qb * 128
            chunks = []
            if qb > 0:
                chunks.append(((qb - 1) * 128, 1))
            chunks.append((qb * 128, 0))
            if qb < 19:
                chunks.append(((qb + 1) * 128, 2))
            chunks += [(WIN, 0), (WIN + 128, 0)]
            nch = len(chunks)

            def A(u):
                ps1, ps1_r = self.ps()
                ps2, ps2_r = self.ps()
                for c, (kt, mk) in enumerate(chunks):
                    pp, pr = (ps1, ps1_r) if c < 4 else (ps2, ps2_r)
                    cc = c % 4
                    self.mm(pp[:, cc * 128:(cc + 1) * 128], pr, kh[kb][:, kt:kt + 128], qh[hb][:, q0:q0 + 128],
                            True, True, reads=(q_r[hb], kv_r[kb][0]))
                mi = 0
                for c, (kt, mk) in enumerate(chunks):
                    pp, pr = (ps1, ps1_r) if c < 4 else (ps2, ps2_r)
                    cc = c % 4
                    src = pp[:, cc * 128:(cc + 1) * 128]
                    if mk == 0:
                        self.act(pT[u][:, c * 128:(c + 1) * 128], src, AF.Exp, reads=(pr,), writes=(pT_r[u],), scale=SCALE)
                    else:
                        mm_ = msk[:, (mk - 1) * 128:mk * 128]
                        sbv = sb[u][:, mi * 128:(mi + 1) * 128]
                        mi += 1
                        self.stt(sbv, src, SCALE, mm_, ALU.mult, ALU.add, reads=(pr, msk_r), writes=(sb_r[u],))
                        self.act(pT[u][:, c * 128:(c + 1) * 128], sbv, AF.Exp, reads=(sb_r[u],), writes=(pT_r[u],))

            def B(u):
                po, po_r = self.ps()
                pz, pz_r = self.ps()
                for c, (kt, mk) in enumerate(chunks):
                    self.mm(po[:, 0:128], po_r, vh[kb][:, kt // 128, :], pT[u][:, c * 128:(c + 1) * 128], c == 0, c == nch - 1,
                            reads=(kv_r[kb][1], pT_r[u]))
                for c in range(nch):
                    self.mm(pz[:, 0:128], pz_r, self.ones_bf, pT[u][:, c * 128:(c + 1) * 128], c == 0, c == nch - 1,
                            reads=(self.ones_res, pT_r[u]))
                self.act(lnb[u], pz[:, 0:128], AF.Ln, reads=(pz_r, esk2_r), writes=(lnb_r[u],), bias=esk[:, h:h + 1])
                self.act(rec[u], lnb[u], AF.Exp, reads=(lnb_r[u],), writes=(rec_r[u],), scale=-1.0)
                self.tt(ost[hb][:, q0:q0 + 128], po[:, 0:128], rec[u], ALU.mult, reads=(po_r, rec_r[u]), writes=(ost_r[hb],))
            return A, B

        def swa_pre(h, hb, kb, kv):
            def f():
                self.load(qh[hb], self.QT[h * 128:(h + 1) * 128, :], q_r[hb], reads=(self.qt_res[h],))
                if h % 4 == 0:
                    self.load(kh[kb], self.KT[kv * 128:(kv + 1) * 128, :], kv_r[kb][0], reads=(self.kt_res[kv],))
                    self.load(vh[kb], self.VV[:, kv * 128:(kv + 1) * 128].rearrange("(b p) d -> p b d", p=128), kv_r[kb][1],
                              reads=tuple(self.vv_res))
            return f

        def swa_post(h, hb):
            def f():
                self.store(self.ST[h * 128:(h + 1) * 128, 0:WIN], ost[hb], ost_r[hb], writes=(self.st_res[h],))
            return f

        units = []
        for h in range(16):
            hb = h % 2
            kv = h // 4
            kb = kv % 2
            for qb in range(16):
                A, B = swa_unit(h, hb, kb, qb)
                units.append((swa_pre(h, hb, kb, kv) if qb == 0 else None, A, B, swa_post(h, hb) if qb == 15 else None))
        self.run_units(units)
        if sstop <= 4:
            return
        self.areset()
        self.xres_setup()
        rb = [self.aalloc(KC * 512, BF16).rearrange("p (k t) -> p k t", t=512) for _ in range(2)]
        rb_r = [P.res(dma=True), P.res(dma=True)]
        self.out_proj(self.swa_w_o, KC, self.rhs_from_dram(self.ST, self.st_res, rb, rb_r), [[0, 1], [2, 3]], gates)

    def program(self):
        self.pers_off = 0
        self.arena_off = 0
        self.ps_i = 0
        self.ws_idx = 0
        self.eps_ap = self.palloc(16)[:, 0:1]
        self.eps_res = self.P.res()
        self.P.op("dve", (lambda e: e.memset(self.eps_ap, EPS)), writes=(self.eps_res,))
        self.setup()
        if self.lo == 0:
            self.transpose_in()
        else:
            r0 = self.P.res(dma=True, persistent=True)
            self.P.dma("sp", (lambda e: e.dma_start(out=self.XT, in_=self.xt_in)), r0, reads=(), writes=tuple(self.xt_res) + (r0,))
            self.P.barrier()
        stop = self.stop_after
        done = False
        TILES[4] = (2048, 512, 0)
        for i in range(self.lo, self.hi):
            last = i == 3
            if last:
                TILES[4] = (2048, 128, 0)
            self.mod_phase(i)
            for ph in ("a", "b", "c"):
                if ph == "a":
                    self.ffn_phase(i, 0, True)
                elif ph == "b":
                    [self.mixer_na, self.mixer_conformer, self.mixer_sconv, self.mixer_swa][i](i)
                else:
                    self.ffn_phase(i, 1, not last)
                self.snap("%d%s" % (i, ph))
                if stop == "%d%s" % (i, ph):
                    done = True
                    break
            if done:
                break
        if self.hi == 4:
            self.final_phase()
        else:
            self.P.barrier()
            r1 = self.P.res(dma=True, persistent=True)
            self.P.dma("sp", (lambda e: e.dma_start(out=self.xt_out, in_=self.XT)), r1, reads=tuple(self.xt_res), writes=(r1,))
        self.P.barrier()
        if self.dbg:
            r = self.P.res(dma=True)
            self.P.dma("sp", (lambda e: e.dma_start(out=self.dbg_xt, in_=self.XT)), r,
                       reads=tuple(self.xt_res), writes=(r,))
        self.P.barrier()

    def build(self):
        from contextlib import ExitStack
        self.P.dry = True
        self.program()
        self.P.dry = False
        self.program()
        with ExitStack() as es:
            self.P.emit(self.nc, es)
        return self.nc


def _na_bias_table(rpb, par):
    H = 16
    tab = np.full((H, 5, 640, 128), NEG, np.float32)
    plist = [0, 1, 2, 18, 19]
    for vi, p in enumerate(plist):
        bs = min(max(2 * p - 4, 0), 30)
        for a in range(2):
            lr = 2 * p + a
            rg = lr if par == 0 else 63 - lr
            rs = min(max(rg - 4, 0), 56)
            for j in range(8):
                krow_g = rs + j
                klr = krow_g if par == 0 else 63 - krow_g
                jb = klr - bs
                if not (0 <= jb < 10):
                    continue
                roff = krow_g - rg + 7
                for qc in range(64):
                    cg = qc if par == 0 else 63 - qc
                    cs = min(max(cg - 8, 0), 48)
                    kc_g = cs + np.arange(16)
                    coff = kc_g - cg + 15
                    klc = kc_g if par == 0 else 63 - kc_g
                    tab[:, vi, jb * 64 + klc, a * 64 + qc] = rpb[:, roff, coff]
    t = tab.reshape(H, 5, 5, 128, 128).transpose(0, 3, 1, 2, 4)
    return np.ascontiguousarray(t.reshape(H, 128, 25 * 128))


def _rope_tables(par):
    inv_freq = np.power(np.float32(10000.0), -np.arange(32, dtype=np.float32) / np.float32(32)).astype(np.float32)
    pos = np.arange(WIN) if par == 0 else 4095 - np.arange(WIN)
    row = (pos // GRID_W).astype(np.float32)
    colp = (pos % GRID_W).astype(np.float32)
    cos = np.zeros((128, WIN), np.float32)
    sin = np.zeros((128, WIN), np.float32)
    for d in range(128):
        half = d // 64
        e = d % 64
        f = e % 32
        ang = (row if half == 0 else colp) * inv_freq[f]
        cos[d] = np.cos(ang)
        sin[d] = -np.sin(ang) if e < 32 else np.sin(ang)
    return cos, sin


def _sw_perm():
    d = np.arange(128)
    e = d % 64
    return np.where(e < 32, d + 32, d - 32)


_NC_CACHE = {}


def _get_nc(lo, hi):
    key = (lo, hi)
    if key not in _NC_CACHE:
        _NC_CACHE[key] = Builder(None, False, lo, hi).build()
    return _NC_CACHE[key]


def _host_inputs(inputs, cores, lo=0, hi=4, xts=None):
    f = lambda a: np.ascontiguousarray(np.asarray(a, dtype=np.float32))
    x, c, ctx, c_ctx = (f(inputs[k]) for k in ("x", "c", "ctx", "c_ctx"))
    perm = _sw_perm()
    wq = f(inputs["swa_w_qkv"][0])
    qcols = (np.arange(16)[:, None] * 128 + perm[None, :]).reshape(-1)
    kcols = 2048 + (np.arange(4)[:, None] * 128 + perm[None, :]).reshape(-1)
    w_sw = np.ascontiguousarray(wq[:, np.concatenate([qcols, kcols])])
    i_ = np.arange(128)
    m_lo = np.where(i_[:, None] >= i_[None, :], 0.0, NEG).astype(np.float32)
    m_hi = np.where(i_[:, None] <= i_[None, :], 0.0, NEG).astype(np.float32)
    shared = {
        "w_mod": f(inputs["w_mod"][lo:hi]), "b_mod": f(inputs["b_mod"][lo:hi]), "norm_g": f(inputs["norm_g"][lo:hi]),
        "ffn_w_in": f(inputs["ffn_w_in"][lo:hi]), "ffn_w_out": f(inputs["ffn_w_out"][lo:hi]),
        "na_w_qkv": f(inputs["na_w_qkv"][0]), "na_w_o": f(inputs["na_w_o"][0]),
        "cv_w_pw1": f(inputs["cv_w_pw1"][0]), "cv_b_pw1": f(inputs["cv_b_pw1"][0]),
        "cv_b_dw": f(inputs["cv_b_dw"][0]), "cv_ln_g": f(inputs["cv_ln_g"][0]), "cv_ln_b": f(inputs["cv_ln_b"][0]),
        "cv_w_pw2": f(inputs["cv_w_pw2"][0]), "cv_b_pw2": f(inputs["cv_b_pw2"][0]),
        "sc_w_in": f(inputs["sc_w_in"][0]), "sc_w_out": f(inputs["sc_w_out"][0]),
        "swa_w_qkv": wq, "swa_w_sw": w_sw, "swa_w_o": f(inputs["swa_w_o"][0]), "swa_sink": np.ascontiguousarray(np.tile(f(inputs["swa_sink"][0])[None, :], (128, 1))),
        "final_g": f(inputs["final_g"]), "ident": np.eye(128, dtype=np.float32),
        "swa_mask": np.ascontiguousarray(np.concatenate([m_lo, m_hi], axis=1)),
    }
    ropes = [_rope_tables(0), _rope_tables(1)]
    rpb = f(inputs["na_rpb"][0])
    nab = [_na_bias_table(rpb, 0), _na_bias_table(rpb, 1)]
    wdw = f(inputs["cv_w_dw"][0])
    wsc = f(inputs["sc_w_conv"][0])
    wdws = [wdw, np.ascontiguousarray(wdw[::-1])]
    wscs = [wsc, np.ascontiguousarray(wsc[::-1])]
    in_maps = []
    for core in cores:
        b, half = core // 2, core % 2
        m = dict(shared)
        if half == 0:
            m["x_win"] = np.ascontiguousarray(x[b, 0:WIN])
            m["ctx_in"] = np.ascontiguousarray(ctx[b])
        else:
            m["x_win"] = np.ascontiguousarray(x[b, ::-1][0:WIN])
            m["ctx_in"] = np.ascontiguousarray(ctx[b, ::-1])
        m["cvec"] = np.ascontiguousarray(np.stack([c[b], c_ctx]))
        m["na_bias"] = nab[half]
        m["cv_w_dw"] = wdws[half]
        m["sc_w_conv"] = wscs[half]
        m["rope_cos"], m["rope_sin"] = ropes[half]
        if xts is not None:
            m["xt_in"] = xts[len(in_maps)]
        in_maps.append(m)
    return in_maps


def kernel(**inputs):
    cores = list(range(8))
    nc = _get_nc(0, 4)
    res = run_bass_kernel_spmd(nc, _host_inputs(inputs, cores, 0, 4), core_ids=cores)
    out = np.empty((4, 4096, D), np.float32)
    for core in range(8):
        b, half = core // 2, core % 2
        o = res.results[core]["out"]
        if half == 0:
            out[b, 0:2048] = o
        else:
            out[b, 2048:4096] = o[::-1]
    return out
```

```python
import numpy as np
import ml_dtypes
import concourse.bass as bass
import concourse.mybir as mybir
from concourse.bass_utils import run_bass_kernel_spmd

F32 = mybir.dt.float32
BF16 = mybir.dt.bfloat16
AF = mybir.ActivationFunctionType
ALU = mybir.AluOpType

D = 2048
KC = 16
FF = 5632
FC = 44
WIN = 2560
NCTX = 256
TT = WIN + NCTX
GRID_W = 64
EPS = 1e-6
SCALE = 128 ** -0.5
NEG = -30000.0
TILES = [(0, 512, 0), (512, 512, 0), (1024, 512, 0), (1536, 512, 0), (2048, 512, 0), (2560, 256, 1)]
NBLK = TT // 128
WSLOT = 5632
NWB = 4


class Res:
    __slots__ = ("w", "r", "sem", "excl")

    def __init__(self, sem=None):
        self.w = None
        self.r = {}
        self.sem = sem
        self.excl = False


class Prog:
    ENG = ("pe", "act", "dve", "pool", "sp")

    def __init__(self):
        self.ops = {e: [] for e in self.ENG}
        self.cnt = {e: 0 for e in self.ENG}
        self.known = {e: {} for e in self.ENG}
        self.dcnt = {}
        self.dry = False
        self.nsem = 0
        self.free_sems = []
        self.phase_sems = []
        self.nobarrier = set()

    def res(self, dma=False, persistent=False):
        r = Res()
        if dma and not self.dry:
            if self.free_sems:
                name = self.free_sems.pop()
            else:
                name = "d%d" % self.nsem
                self.nsem += 1
                self.dcnt[name] = 0
            r.sem = name
            if not persistent:
                self.phase_sems.append(name)
        return r

    def release_phase(self):
        self.free_sems.extend(self.phase_sems)
        self.phase_sems = []

    def _need(self, eng, waits, s, v):
        if s == "pe" and eng == "pe":
            return
        if self.known[eng].get(s, 0) >= v:
            return
        if waits.get(s, 0) < v:
            waits[s] = v

    def _deps(self, eng, reads, writes):
        waits = {}
        for r in reads:
            if r.w is not None:
                self._need(eng, waits, *r.w)
        for w in writes:
            if w.w is not None:
                self._need(eng, waits, *w.w)
            for s, v in w.r.items():
                self._need(eng, waits, s, v)
        for s, v in waits.items():
            self.known[eng][s] = v
        return tuple(waits.items())

    def _mark(self, tag, reads, writes):
        s, v = tag
        for r in reads:
            if r.r.get(s, 0) < v:
                r.r[s] = v
        for w in writes:
            w.w = tag
            w.r = {}

    def op(self, eng, fn, reads=(), writes=()):
        if self.dry:
            return
        ex = tuple(r for r in reads if r.excl)
        if ex:
            writes = tuple(writes) + ex
            reads = tuple(r for r in reads if not r.excl)
        waits = self._deps(eng, reads, writes)
        self.cnt[eng] += 1
        self.ops[eng].append((waits, fn, (eng, 1)))
        self._mark((eng, self.cnt[eng]), reads, writes)

    def dma(self, q, fn, sres, reads=(), writes=()):
        if self.dry:
            return
        sem = sres.sem
        waits = dict(self._deps(q, reads, writes))
        if self.dcnt[sem] > 0:
            self._need(q, waits, sem, self.dcnt[sem])
            self.known[q][sem] = max(self.known[q].get(sem, 0), self.dcnt[sem])
        self.dcnt[sem] += 16
        self.ops[q].append((tuple(waits.items()), fn, (sem, 16)))
        self._mark((sem, self.dcnt[sem]), reads, writes)

    def barrier(self):
        if self.dry:
            return
        tot = dict(self.cnt)
        tot.update(self.dcnt)
        tot.pop("pool", None)
        for sname in self.nobarrier:
            tot.pop(sname, None)
        for e in self.ENG:
            if e == "pool":
                continue
            waits = {}
            for s, v in tot.items():
                if v > 0 and s != e:
                    self._need(e, waits, s, v)
            for s, v in waits.items():
                self.known[e][s] = v
            if waits:
                self.ops[e].append((tuple(waits.items()), None, None))

    def emit(self, nc, es):
        names = list(self.ENG) + list(self.dcnt.keys())
        sems = {n: es.enter_context(nc.semaphore("s_" + n)) for n in names}
        block = es.enter_context(nc.Block())

        def mk(en):
            def body(e):
                for waits, fn, inc in self.ops[en]:
                    for s, v in waits:
                        e.wait_ge(sems[s], v)
                    if fn is not None:
                        fn(e).then_inc(sems[inc[0]], inc[1])
            return body

        block.tensor(mk("pe"))
        block.scalar(mk("act"))
        block.vector(mk("dve"))
        block.gpsimd(mk("pool"))
        block.sync(mk("sp"))


class Builder:
    def __init__(self, stop_after=None, dbg=False, lo=0, hi=4):
        self.lo, self.hi = lo, hi
        nl = hi - lo
        self.stop_after = stop_after
        self.dbg = dbg
        nc = self.nc = bass.Bass("TRN2", target_bir_lowering=False)
        P = self.P = Prog()

        def din(name, shape, dt=F32):
            return nc.dram_tensor(name, list(shape), dt, kind="ExternalInput").ap()

        if lo == 0:
            self.x_win = din("x_win", [WIN, D])
            self.ctx_in = din("ctx_in", [NCTX, D])
        else:
            self.xt_in = din("xt_in", [D, TT])
        self.cvec = din("cvec", [2, D])
        self.w_mod = din("w_mod", [nl, D, 9 * D])
        self.b_mod = din("b_mod", [nl, 9 * D])
        self.norm_g = din("norm_g", [nl, 3, D])
        self.ffn_w_in = din("ffn_w_in", [nl, 2, D, 2 * FF])
        self.ffn_w_out = din("ffn_w_out", [nl, 2, FF, D])
        if lo <= 0 < hi:
            self.na_w_qkv = din("na_w_qkv", [D, 3 * D])
            self.na_w_o = din("na_w_o", [D, D])
            self.na_bias = din("na_bias", [16, 128, 25 * 128])
        if lo <= 1 < hi:
          self.cv_w_pw1 = din("cv_w_pw1", [D, 2 * D])
          self.cv_b_pw1 = din("cv_b_pw1", [2 * D])
          self.cv_w_dw = din("cv_w_dw", [31, D])
          self.cv_b_dw = din("cv_b_dw", [D])
          self.cv_ln_g = din("cv_ln_g", [D])
          self.cv_ln_b = din("cv_ln_b", [D])
          self.cv_w_pw2 = din("cv_w_pw2", [D, D])
          self.cv_b_pw2 = din("cv_b_pw2", [D])
        if lo <= 2 < hi:
            self.sc_w_in = din("sc_w_in", [D, 3 * D])
            self.sc_w_conv = din("sc_w_conv", [3, D])
            self.sc_w_out = din("sc_w_out", [D, D])
        if lo <= 3 < hi:
            self.swa_w_qkv = din("swa_w_qkv", [D, 3072])
            self.swa_w_sw = din("swa_w_sw", [D, 2560])
            self.swa_w_o = din("swa_w_o", [D, D])
            self.swa_sink = din("swa_sink", [128, 16])
            self.rope_cos = din("rope_cos", [128, WIN])
            self.rope_sin = din("rope_sin", [128, WIN])
            self.swa_mask = din("swa_mask", [128, 256])
        self.ident_in = din("ident", [128, 128])
        if hi == 4:
            self.final_g = din("final_g", [D])
            self.out = nc.dram_tensor("out", [2048, D], F32, kind="ExternalOutput").ap()
        else:
            self.xt_out = nc.dram_tensor("xt_out", [D, TT], F32, kind="ExternalOutput").ap()
        if dbg:
            self.dbg_xt = nc.dram_tensor("dbg_xt", [D, TT], F32, kind="ExternalOutput").ap()

        def dscr(name, shape, dt):
            return nc.dram_tensor(name, list(shape), dt, kind="Internal").ap()

        self.XT = dscr("XT", [D, TT], F32)
        self.CT = dscr("CT", [D, TT], F32)
        self.ST = dscr("ST", [D, TT], BF16)
        self.QT = dscr("QT", [D, TT], BF16)
        self.KT = dscr("KT", [D, TT], BF16)
        self.VV = dscr("VV", [TT, D], BF16)
        self.xt_res = [P.res() for _ in TILES]
        self.ct_res = [P.res() for _ in range(KC)]
        self.st_res = [P.res() for _ in range(KC)]
        self.qt_res = [P.res() for _ in range(KC)]
        self.kt_res = [P.res() for _ in range(KC)]
        self.vv_res = [P.res() for _ in range(NBLK)]
        self.out_res = P.res()

        self.wring_t = nc.alloc_sbuf_tensor("wring", [128, NWB * WSLOT], BF16)
        self.wring = self.wring_t.ap()
        self.wres = [P.res(dma=True, persistent=True) for _ in range(NWB)]
        for r_ in self.wres:
            P.nobarrier.add(r_.sem)
        self.pers_t = nc.alloc_sbuf_tensor("pers", [128, 2048], F32)
        self.pers = self.pers_t.ap()
        self.pers_off = 0
        ARENA_F32 = 37 * 1024
        self.arena_t = nc.alloc_sbuf_tensor("arena", [128, ARENA_F32], F32)
        self.arena = self.arena_t.ap()
        self.arena_cap = ARENA_F32 * 4
        self.arena_off = 0
        self.psum = []
        for i in range(8):
            t = nc.alloc_psum_tensor("ps%d" % i, [128, 512], F32)
            pr_ = P.res()
            pr_.excl = True
            self.psum.append((t.ap(), pr_))
        self.ps_i = 0
        self.ws_descs = []
        self.ws_idx = 0
        self.ws_issued = 0
        self.hold = 2
        self.dumped = set()

    def palloc(self, nf32):
        a = self.pers[:, self.pers_off:self.pers_off + nf32]
        self.pers_off += nf32
        assert self.pers_off <= 2048
        return a

    def areset(self):
        self.P.barrier()
        self.P.release_phase()
        self.arena_off = 0

    def reclaim_u(self):
        self.P.barrier()
        self.arena_off = self.after_u

    def aalloc(self, nelem, dt=F32):
        nbytes = nelem * (4 if dt == F32 else 2)
        nbytes = (nbytes + 63) // 64 * 64
        assert self.arena_off + nbytes <= self.arena_cap, (self.arena_off, nbytes)
        a = self.arena[:, self.arena_off // 4:(self.arena_off + nbytes) // 4]
        self.arena_off += nbytes
        if dt != F32:
            a = a.bitcast(dt)
        return a[:, 0:nelem]

    def ps(self):
        ap, r = self.psum[self.ps_i]
        self.ps_i = (self.ps_i + 1) % 8
        return ap, r

    def wget(self, src, KG, NW):
        i = self.ws_idx
        self.ws_idx += 1
        slot = i % NWB
        view = self.wring[:, slot * WSLOT: slot * WSLOT + KG * NW].rearrange("p (k n) -> p k n", n=NW)
        if self.P.dry:
            self.ws_descs.append((src, KG, NW))
            return view, self.wres[slot]
        while self.ws_issued < min(len(self.ws_descs), i + NWB - self.hold):
            j = self.ws_issued
            s_src, s_kg, s_nw = self.ws_descs[j]
            sl = j % NWB
            dst = self.wring[:, sl * WSLOT: sl * WSLOT + s_kg * s_nw].rearrange("p (k n) -> p k n", n=s_nw)
            srcv = s_src.rearrange("(kc p) n -> p kc n", p=128)
            self.P.dma("pool", (lambda e, d=dst, s=srcv: e.dma_start(out=d, in_=s)),
                       self.wres[sl], reads=(), writes=(self.wres[sl],))
            self.ws_issued += 1
        return view, self.wres[slot]

    def mm(self, ps, psr, lhsT, rhs, start, stop, reads):
        self.P.op("pe", (lambda e: e.matmul(ps, lhsT=lhsT, rhs=rhs, start=start, stop=stop)),
                  reads=reads, writes=(psr,))

    def act(self, out, in_, func, reads, writes, bias=None, scale=None):
        kw = {}
        if bias is not None:
            kw["bias"] = bias
        if scale is not None:
            kw["scale"] = scale
        self.P.op("act", (lambda e: e.activation(out=out, in_=in_, func=func, **kw)), reads=reads, writes=writes)

    def tt(self, out, in0, in1, op, reads, writes, eng="dve"):
        self.P.op(eng, (lambda e: e.tensor_tensor(out=out, in0=in0, in1=in1, op=op)), reads=reads, writes=writes)

    def ts(self, out, in0, s1, op0, reads, writes, s2=None, op1=None, eng="dve"):
        if op1 is None:
            self.P.op(eng, (lambda e: e.tensor_scalar(out=out, in0=in0, scalar1=s1, scalar2=None, op0=op0)),
                      reads=reads, writes=writes)
        else:
            self.P.op(eng, (lambda e: e.tensor_scalar(out=out, in0=in0, scalar1=s1, scalar2=s2, op0=op0, op1=op1)),
                      reads=reads, writes=writes)

    def stt(self, out, in0, scalar, in1, op0, op1, reads, writes, eng="dve"):
        self.P.op(eng, (lambda e: e.scalar_tensor_tensor(out=out, in0=in0, scalar=scalar, in1=in1, op0=op0, op1=op1)),
                  reads=reads, writes=writes)

    def copy(self, out, in_, reads, writes, eng="dve"):
        if eng == "act":
            self.P.op("act", (lambda e: e.activation(out=out, in_=in_, func=AF.Copy)), reads=reads, writes=writes)
        else:
            self.P.op(eng, (lambda e: e.tensor_copy(out=out, in_=in_)), reads=reads, writes=writes)

    def load(self, out, in_, sres, reads=(), q="sp", slow=False):
        if slow:
            fn = (lambda e: e.dma_start(out=out, in_=in_, allow_slow_non_contiguous=True))
        else:
            fn = (lambda e: e.dma_start(out=out, in_=in_))
        self.P.dma(q, fn, sres, reads=reads, writes=(sres,))

    def store(self, out, in_, sres, writes, q="sp"):
        self.P.dma(q, (lambda e: e.dma_start(out=out, in_=in_)), sres, reads=(sres,), writes=writes)

    def dump(self, name, ap, reads):
        if not self.dbg or self.P.dry or name in self.dumped:
            return
        self.dumped.add(name)
        shp = [int(x) for x in ap.shape]
        n = 1
        for x in shp[1:]:
            n *= x
        t = self.nc.dram_tensor("dump_" + name, [shp[0], n], ap.dtype, kind="ExternalOutput").ap()
        r = self.P.res(dma=True, persistent=True)
        src = ap
        if len(shp) == 3:
            t = t.rearrange("p (a b) -> p a b", b=shp[2])
        self.P.dma("sp", (lambda e: e.dma_start(out=t, in_=src)), r, reads=tuple(reads), writes=(r,))

    def snap(self, name):
        if not self.dbg or self.P.dry:
            return
        t = self.nc.dram_tensor("snap_" + name, [D, TT], F32, kind="ExternalOutput").ap()
        r = self.P.res(dma=True, persistent=True)
        self.P.barrier()
        self.P.dma("sp", (lambda e: e.dma_start(out=t, in_=self.XT)), r, reads=tuple(self.xt_res), writes=(r,))
        self.P.barrier()

    def load_vec(self, dst, src_vec, sres):
        self.load(dst, src_vec.rearrange("(k p) -> p k", p=128), sres, slow=True)

    def setup(self):
        P = self.P
        self.const_res = P.res(dma=True, persistent=True)
        self.ident = self.palloc(128)
        self.load(self.ident, self.ident_in, self.const_res)
        self.ident_bf = self.palloc(64).bitcast(BF16)
        self.identbf_res = P.res()
        self.copy(self.ident_bf, self.ident, reads=(self.const_res,), writes=(self.identbf_res,))
        self.ones_bf = self.palloc(64).bitcast(BF16)
        self.ones_res = P.res()
        P.op("dve", (lambda e: e.memset(self.ones_bf, 1.0)), writes=(self.ones_res,))
        self.ng = self.palloc(12 * 16)
        self.ng_res = P.res(dma=True, persistent=True)
        for i in range(self.hi - self.lo):
            for j in range(3):
                o = (i * 3 + j) * 16
                self.load_vec(self.ng[:, o:o + 16], self.norm_g[i, j], self.ng_res)
        self.fg = self.palloc(16)
        if self.hi == 4:
            self.load_vec(self.fg, self.final_g, self.ng_res)
        cv = self.palloc(32)
        self.cv_res = P.res(dma=True, persistent=True)
        for r in range(2):
            self.load_vec(cv[:, r * 16:(r + 1) * 16], self.cvec[r], self.cv_res)
        self.sc_bf = self.palloc(16).bitcast(BF16)
        self.sc_res = P.res()
        scv = self.sc_bf.rearrange("p (k r) -> p k r", r=2)
        for r in range(2):
            self.act(scv[:, :, r], cv[:, r * 16:(r + 1) * 16], AF.Silu, reads=(self.cv_res,), writes=(self.sc_res,))
        self.bm = self.palloc(144)
        self.bm_res = P.res(dma=True, persistent=True)
        self.modv = [self.palloc(144), self.palloc(144)]
        self.mod_res = P.res()
        self.der = self.palloc(16 * 8)
        self.der_res = P.res()
        self.vecs = self.palloc(16 * 40)
        self.vec_res = P.res(dma=True, persistent=True)
        self.vec2_res = P.res()

    def transpose_in(self):
        P = self.P
        self.areset()
        tin = [self.aalloc(D), self.aalloc(D)]
        tin_r = [P.res(dma=True), P.res(dma=True)]
        tout = [self.aalloc(D), self.aalloc(D)]
        tout_r = [P.res(dma=True), P.res(dma=True)]
        for blk in range(NBLK):
            b = blk % 2
            src = self.x_win[blk * 128:(blk + 1) * 128, :] if blk < 20 else self.ctx_in[(blk - 20) * 128:(blk - 19) * 128, :]
            self.load(tin[b], src, tin_r[b])
            for q in range(4):
                ps, psr = self.ps()
                for c in range(4):
                    k = q * 4 + c
                    o, i_ = ps[:, c * 128:(c + 1) * 128], tin[b][:, k * 128:(k + 1) * 128]
                    P.op("pe", (lambda e, o=o, i_=i_: e.transpose(o, i_, self.ident)),
                         reads=(tin_r[b], self.const_res), writes=(psr,))
                self.copy(tout[b][:, q * 512:(q + 1) * 512], ps, reads=(psr,), writes=(tout_r[b],),
                          eng=("act" if q % 2 else "dve"))
            ti = min(blk // 4, 5)
            self.store(self.XT[:, blk * 128:(blk + 1) * 128].rearrange("(k p) t -> p k t", p=128),
                       tout[b].rearrange("p (k t) -> p k t", t=128), tout_r[b], writes=(self.xt_res[ti],))

    def mod_phase(self, i):
        P = self.P
        for c0 in range(0, 144, 16):
            self.load(self.bm[:, c0:c0 + 16], self.b_mod[i - self.lo, c0 * 128:(c0 + 16) * 128].rearrange("(k p) -> p k", p=128),
                      self.bm_res, slow=True)
        ps, psr = self.ps()
        sc = self.sc_bf.rearrange("p (k r) -> p k r", r=2)
        for sl in range(72):
            slab, sres = self.wget(self.w_mod[i - self.lo, :, sl * 256:(sl + 1) * 256], 16, 256)
            for cc in range(2):
                c = sl * 2 + cc
                for k in range(KC):
                    self.mm(ps[:, 2 * c:2 * c + 2], psr, slab[:, k, cc * 128:(cc + 1) * 128], sc[:, k, :],
                            k == 0, k == KC - 1, reads=(sres, self.sc_res))
        pv = ps[:, 0:288].rearrange("p (c r) -> p c r", r=2)
        for v in range(2):
            self.tt(self.modv[v], pv[:, :, v], self.bm, ALU.add, reads=(psr, self.bm_res), writes=(self.mod_res,))
        self.dump("modv0", self.modv[0], (self.mod_res,))
        self.dump("modv1", self.modv[1], (self.mod_res,))
        self.dump("bm", self.bm, (self.bm_res,))
        self.dump("scbf", self.sc_bf, (self.sc_res,))
        self.dump("ng", self.ng, (self.ng_res,))

    def mslot(self, v, s):
        return self.modv[v][:, s * 16:(s + 1) * 16]

    def derive(self, i, gidx, s_shift, s_scale, s_gate, half):
        o = []
        il = i - self.lo
        g = self.ng[:, (il * 3 + gidx) * 16:(il * 3 + gidx) * 16 + 16]
        for v in range(2):
            A = self.der[:, (v * 4) * 16:(v * 4 + 1) * 16]
            G = self.der[:, (v * 4 + 1) * 16:(v * 4 + 2) * 16]
            self.stt(A, self.mslot(v, s_scale), 1.0, g, ALU.add, ALU.mult,
                     reads=(self.mod_res, self.ng_res), writes=(self.der_res,))
            self.ts(G, self.mslot(v, s_gate), 0.5 if half else 1.0, ALU.mult, reads=(self.mod_res,), writes=(self.der_res,))
            o.append((A, self.mslot(v, s_shift), G))
        return o

    def norm_setup(self):
        P = self.P
        self.sq = [self.aalloc(512, BF16), self.aalloc(512, BF16)]
        self.sq_r = [P.res(), P.res()]
        self.srt = self.aalloc(512)
        self.srt_r = P.res()
        self.rstd = self.aalloc(512)
        self.rstd_r = P.res()
        self.tmpn = [self.aalloc(512), self.aalloc(512)]
        self.tmpn_r = [P.res(), P.res()]

    def rms_stats(self, xs, xs_r, n):
        ps, psr = self.ps()
        for k in range(KC):
            b = k % 2
            self.act(self.sq[b][:, :n], xs[:, k, :n], AF.Square, reads=(xs_r,), writes=(self.sq_r[b],))
            self.mm(ps[:, :n], psr, self.ones_bf, self.sq[b][:, :n], k == 0, k == KC - 1,
                    reads=(self.sq_r[b], self.ones_res))
        self.act(self.srt[:, :n], ps[:, :n], AF.Sqrt, reads=(psr,), writes=(self.srt_r,), bias=self.eps_ap, scale=1.0 / D)
        self.P.op("dve", (lambda e, o=self.rstd[:, :n], i_=self.srt[:, :n]: e.reciprocal(out=o, in_=i_)),
                  reads=(self.srt_r,), writes=(self.rstd_r,))

    def norm_tile(self, tile_i, xs, xs_r, A, B, udst, udst_r):
        t0, n, v = TILES[tile_i]
        self.load(xs[:, :, :n], self.XT[:, t0:t0 + n].rearrange("(k p) t -> p k t", p=128), xs_r,
                  reads=(self.xt_res[tile_i],))
        self.rms_stats(xs, xs_r, n)
        for k in range(KC):
            b = k % 2
            self.tt(self.tmpn[b][:, :n], xs[:, k, :n], self.rstd[:, :n], ALU.mult,
                    reads=(xs_r, self.rstd_r), writes=(self.tmpn_r[b],))
            self.act(udst[:, k, :n], self.tmpn[b][:, :n], AF.Identity, reads=(self.tmpn_r[b], self.der_res, self.mod_res),
                     writes=(udst_r,), bias=B[:, k:k + 1], scale=A[:, k:k + 1])

    def out_proj(self, W, KG, rhs_fn, groups, gates, bias=None):
        P = self.P
        for grp in groups:
            rv, rres = rhs_fn(grp)
            for m in range(KC):
                slab, sres = self.wget(W[:, m * 128:(m + 1) * 128], KG, 128)
                for s, ti in enumerate(grp):
                    t0, n, v = TILES[ti]
                    b = self.xr_i
                    self.xr_i = (self.xr_i + 1) % 4
                    xr, xr_r = self.xres[b], self.xres_r[b]
                    self.load(xr[:, :n], self.XT[m * 128:(m + 1) * 128, t0:t0 + n], xr_r, reads=(self.xt_res[ti],))
                    ps, psr = self.ps()
                    for j in range(KG):
                        self.mm(ps[:, :n], psr, slab[:, j, :], rv(s, j, n), j == 0, j == KG - 1, reads=(sres,) + tuple(rres))
                    G = gates[v]
                    if bias is not None:
                        self.stt(xr[:, :n], ps[:, :n], G[:, m:m + 1], xr[:, :n], ALU.mult, ALU.add,
                                 reads=(psr, self.der_res, self.mod_res), writes=(xr_r,))
                        self.ts(xr[:, :n], xr[:, :n], bias[v][:, m:m + 1], ALU.add, reads=(self.vec2_res,), writes=(xr_r,))
                    else:
                        self.stt(xr[:, :n], ps[:, :n], G[:, m:m + 1], xr[:, :n], ALU.mult, ALU.add,
                                 reads=(psr, self.der_res, self.mod_res), writes=(xr_r,))
                    self.store(self.XT[m * 128:(m + 1) * 128, t0:t0 + n], xr[:, :n], xr_r, writes=(self.xt_res[ti],))

    def xres_setup(self):
        self.xres = [self.aalloc(512) for _ in range(4)]
        self.xres_r = [self.P.res(dma=True) for _ in range(4)]
        self.xr_i = 0

    def ffn_phase(self, i, which, with_ctx):
        P = self.P
        self.areset()
        uT = self.aalloc(KC * 1024, BF16).rearrange("p (k t) -> p k t", t=1024)
        u_r = [P.res(), P.res()]
        actT = self.aalloc(FC * 1024, BF16)
        act_r = [P.res(dma=True), P.res(dma=True)]
        act_v = [actT[:, s * FC * 512:(s + 1) * FC * 512].rearrange("p (j t) -> p j t", t=512) for s in range(2)]
        xs_v = [actT[:, s * FC * 512: s * FC * 512 + KC * 512 * 2].bitcast(F32).rearrange("p (k t) -> p k t", t=512)
                for s in range(2)]
        self.norm_setup()
        self.xres_setup()
        sg = [self.aalloc(512), self.aalloc(512)]
        sg_r = [P.res(), P.res()]
        sgi = 0
        Wi = self.ffn_w_in[i - self.lo, which]
        Wo = self.ffn_w_out[i - self.lo, which]
        if which == 0:
            dv = self.derive(i, 0, 0, 1, 2, True)
        else:
            dv = self.derive(i, 2, 6, 7, 8, True)
        groups = [[0, 1], [2, 3], [4, 5]] if with_ctx else [[0, 1], [2, 3]]
        gates = [dv[0][2], dv[1][2]]

        def rhs_fn(grp):
            nonlocal sgi
            for s, ti in enumerate(grp):
                t0, n, v = TILES[ti]
                A, B, G = dv[v]
                self.norm_tile(ti, xs_v[s], act_r[s], A, B, uT[:, :, s * 512:s * 512 + n], u_r[s])
                self.dump("rstd%d" % s, self.rstd, (self.rstd_r,))
                self.dump("srt%d" % s, self.srt, (self.srt_r,))
                self.dump("xs%d" % s, xs_v[s], (act_r[s],))
            self.dump("uT", uT, (u_r[0], u_r[1]))
            self.dump("der", self.der, (self.der_res,))
            for jp in range(FC // 2):
                slabG, rG = self.wget(Wi[:, jp * 256:(jp + 1) * 256], KC, 256)
                slabU, rU = self.wget(Wi[:, FF + jp * 256:FF + (jp + 1) * 256], KC, 256)
                for jj in range(2):
                    j = jp * 2 + jj
                    for s, ti in enumerate(grp):
                        t0, n, v = TILES[ti]
                        psg, psg_r = self.ps()
                        psu, psu_r = self.ps()
                        for k in range(KC):
                            self.mm(psg[:, :n], psg_r, slabG[:, k, jj * 128:(jj + 1) * 128], uT[:, k, s * 512:s * 512 + n],
                                    k == 0, k == KC - 1, reads=(rG, u_r[s]))
                        for k in range(KC):
                            self.mm(psu[:, :n], psu_r, slabU[:, k, jj * 128:(jj + 1) * 128], uT[:, k, s * 512:s * 512 + n],
                                    k == 0, k == KC - 1, reads=(rU, u_r[s]))
                        b = sgi
                        sgi = (sgi + 1) % 2
                        self.act(sg[b][:, :n], psg[:, :n], AF.Silu, reads=(psg_r,), writes=(sg_r[b],))
                        self.tt(act_v[s][:, j, :n], sg[b][:, :n], psu[:, :n], ALU.mult,
                                reads=(sg_r[b], psu_r), writes=(act_r[s],))
                        if j == 1:
                            self.dump("act%d" % s, act_v[s][:, 0:2, :], (act_r[s],))
            return (lambda s, j, n: act_v[s][:, j, :n]), (act_r[0], act_r[1])

        self.out_proj(Wo, FC, rhs_fn, groups, gates)

    def final_phase(self):
        P = self.P
        self.areset()
        self.norm_setup()
        xs = [self.aalloc(KC * 128).rearrange("p (k t) -> p k t", t=128) for _ in range(2)]
        xs_r = [P.res(dma=True), P.res(dma=True)]
        yb = [self.aalloc(128), self.aalloc(128)]
        yb_r = [P.res(), P.res()]
        tout = [self.aalloc(D), self.aalloc(D)]
        tout_r = [P.res(dma=True), P.res(dma=True)]
        for blk in range(16):
            b = blk % 2
            ti = blk // 4
            self.load(xs[b], self.XT[:, blk * 128:(blk + 1) * 128].rearrange("(k p) t -> p k t", p=128), xs_r[b],
                      reads=(self.xt_res[ti],))
            self.rms_stats(xs[b], xs_r[b], 128)
            for q in range(4):
                ps, psr = self.ps()
                for c in range(4):
                    k = q * 4 + c
                    yy = yb[k % 2]
                    self.stt(yy, xs[b][:, k, :], self.fg[:, k:k + 1], self.rstd[:, :128], ALU.mult, ALU.mult,
                             reads=(xs_r[b], self.rstd_r, self.ng_res), writes=(yb_r[k % 2],))
                    o = ps[:, c * 128:(c + 1) * 128]
                    P.op("pe", (lambda e, o=o, yy=yy: e.transpose(o, yy, self.ident)),
                         reads=(yb_r[k % 2], self.const_res), writes=(psr,))
                self.copy(tout[b][:, q * 512:(q + 1) * 512], ps, reads=(psr,), writes=(tout_r[b],),
                          eng=("act" if q % 2 else "dve"))
            self.store(self.out[blk * 128:(blk + 1) * 128, :], tout[b], tout_r[b], writes=(self.out_res,))

    def mixer_u(self, i, tiles=range(6)):
        P = self.P
        self.uall = self.aalloc(KC * TT, BF16).rearrange("p (k t) -> p k t", t=TT)
        self.uall_r = P.res()
        self.after_u = self.arena_off
        xs = self.aalloc(KC * 512).rearrange("p (k t) -> p k t", t=512)
        xs_r = P.res(dma=True)
        self.norm_setup()
        dv = self.derive(i, 1, 3, 4, 5, False)
        for ti in tiles:
            t0, n, v = TILES[ti]
            A, B, G = dv[v]
            self.norm_tile(ti, xs, xs_r, A, B, self.uall[:, :, t0:t0 + n], self.uall_r)
        self.reclaim_u()
        return [dv[0][2], dv[1][2]]

    def rhs_from_dram(self, SRC, src_res, rb, rb_r):
        def rhs_fn(grp):
            for s, ti in enumerate(grp):
                t0, n, v = TILES[ti]
                self.load(rb[s][:, :, :n], SRC[:, t0:t0 + n].rearrange("(k p) t -> p k t", p=128), rb_r[s],
                          reads=tuple(src_res))
            return (lambda s, j, n: rb[s][:, j, :n]), (rb_r[0], rb_r[1])
        return rhs_fn

    def mixer_sconv(self, i):
        P = self.P
        self.areset()
        gates = self.mixer_u(i)
        wc = self.vecs[:, 0:48]
        for k in range(3):
            self.load_vec(wc[:, k * 16:(k + 1) * 16], self.sc_w_conv[k], self.vec_res)
        vb = self.aalloc(WIN + 2)
        vc = self.aalloc(NCTX + 2)
        v_r = P.res()
        bgb = self.aalloc(TT)
        bg_r = P.res()
        acc = self.aalloc(TT)
        acc_r = P.res()
        tmpc = [self.aalloc(512), self.aalloc(512)]
        tmpc_r = [P.res(), P.res()]
        sst = [self.aalloc(TT, BF16), self.aalloc(TT, BF16)]
        sst_r = [P.res(dma=True), P.res(dma=True)]
        P.op("dve", (lambda e: e.memset(vb, 0.0)), writes=(v_r,))
        P.op("dve", (lambda e: e.memset(vc, 0.0)), writes=(v_r,))
        W = self.sc_w_in
        ci = 0
        for m in range(KC):
            sb_, rb_ = self.wget(W[:, m * 128:(m + 1) * 128], KC, 128)
            sc_, rc_ = self.wget(W[:, D + m * 128:D + (m + 1) * 128], KC, 128)
            sx_, rx_ = self.wget(W[:, 2 * D + m * 128:2 * D + (m + 1) * 128], KC, 128)
            for ti, (t0, n, v) in enumerate(TILES):
                pb, pb_r = self.ps()
                pc, pc_r = self.ps()
                px, px_r = self.ps()
                for (pp, pr, sl, rr) in ((pb, pb_r, sb_, rb_), (pc, pc_r, sc_, rc_), (px, px_r, sx_, rx_)):
                    for k in range(KC):
                        self.mm(pp[:, :n], pr, sl[:, k, :], self.uall[:, k, t0:t0 + n], k == 0, k == KC - 1,
                                reads=(rr, self.uall_r))
                b = ci % 2
                ci += 1
                self.copy(tmpc[b][:, :n], pc[:, :n], reads=(pc_r,), writes=(tmpc_r[b],), eng="act")
                dstv = vb[:, 1 + t0:1 + t0 + n] if v == 0 else vc[:, 1:1 + n]
                self.tt(dstv, tmpc[b][:, :n], px[:, :n], ALU.mult, reads=(tmpc_r[b], px_r), writes=(v_r,))
                self.copy(bgb[:, t0:t0 + n], pb[:, :n], reads=(pb_r,), writes=(bg_r,), eng="act")
            for (src, N, o0) in ((vb, 2048 + TILES[4][1], 0), (vc, NCTX, WIN)):
                a = acc[:, o0:o0 + N]
                self.ts(a, src[:, 0:N], wc[:, m:m + 1], ALU.mult, reads=(v_r, self.vec_res), writes=(acc_r,))
                self.stt(a, src[:, 1:1 + N], wc[:, 16 + m:17 + m], a, ALU.mult, ALU.add, reads=(v_r, self.vec_res), writes=(acc_r,))
                self.stt(a, src[:, 2:2 + N], wc[:, 32 + m:33 + m], a, ALU.mult, ALU.add, reads=(v_r, self.vec_res), writes=(acc_r,))
            b = m % 2
            self.tt(sst[b], acc, bgb, ALU.mult, reads=(acc_r, bg_r), writes=(sst_r[b],))
            self.store(self.ST[m * 128:(m + 1) * 128, :], sst[b], sst_r[b], writes=(self.st_res[m],))
        self.areset()
        self.xres_setup()
        rb = [self.aalloc(KC * 512, BF16).rearrange("p (k t) -> p k t", t=512) for _ in range(2)]
        rb_r = [P.res(dma=True), P.res(dma=True)]
        self.out_proj(self.sc_w_out, KC, self.rhs_from_dram(self.ST, self.st_res, rb, rb_r),
                      [[0, 1], [2, 3], [4, 5]], gates)

    def mixer_conformer(self, i):
        P = self.P
        self.areset()
        gates = self.mixer_u(i)
        V = self.vecs
        wdw = V[:, 0:31 * 16]
        for k in range(31):
            self.load_vec(wdw[:, k * 16:(k + 1) * 16], self.cv_w_dw[k], self.vec_res)
        o = 31 * 16
        b1 = V[:, o:o + 32]
        self.load_vec(b1, self.cv_b_pw1, self.vec_res)
        bdw = V[:, o + 32:o + 48]
        self.load_vec(bdw, self.cv_b_dw, self.vec_res)
        lng = V[:, o + 48:o + 64]
        self.load_vec(lng, self.cv_ln_g, self.vec_res)
        lnb = V[:, o + 64:o + 80]
        self.load_vec(lnb, self.cv_ln_b, self.vec_res)
        b2 = V[:, o + 80:o + 96]
        self.load_vec(b2, self.cv_b_pw2, self.vec_res)
        b2g = [V[:, o + 96:o + 112], V[:, o + 112:o + 128]]
        for v in range(2):
            self.tt(b2g[v], b2, gates[v], ALU.mult, reads=(self.vec_res, self.der_res, self.mod_res), writes=(self.vec2_res,))
        zbs = [self.aalloc(WIN + 32, BF16), self.aalloc(WIN + 32, BF16)]
        zcs = [self.aalloc(NCTX + 32, BF16), self.aalloc(NCTX + 32, BF16)]
        z_rs = [P.res(), P.res()]
        dgs = [self.aalloc(31 * 128, BF16).rearrange("p (k c) -> p k c", c=128) for _ in range(2)]
        dg_rs = [P.res(), P.res()]
        acc = [self.aalloc(TT), self.aalloc(TT)]
        acc_r = [P.res(dma=True), P.res(dma=True)]
        sgm = [self.aalloc(512), self.aalloc(512)]
        sgm_r = [P.res(), P.res()]
        for q_ in range(2):
            P.op("dve", (lambda e, t_=zbs[q_]: e.memset(t_, 0.0)), writes=(z_rs[q_],))
            P.op("dve", (lambda e, t_=zcs[q_]: e.memset(t_, 0.0)), writes=(z_rs[q_],))
        W = self.cv_w_pw1
        ci = 0
        for m in range(KC):
            sa_, ra_ = self.wget(W[:, m * 128:(m + 1) * 128], KC, 128)
            sg_, rg_ = self.wget(W[:, D + m * 128:D + (m + 1) * 128], KC, 128)
            zb, zc, z_r = zbs[m % 2], zcs[m % 2], z_rs[m % 2]
            dg, dg_r = dgs[m % 2], dg_rs[m % 2]
            for k in range(31):
                self.ts(dg[:, k, :], self.ident_bf, wdw[:, k * 16 + m:k * 16 + m + 1], ALU.mult,
                        reads=(self.identbf_res, self.vec_res), writes=(dg_r,))
            for ti, (t0, n, v) in enumerate(TILES):
                pa, pa_r = self.ps()
                pg, pg_r = self.ps()
                for (pp, pr, sl, rr) in ((pa, pa_r, sa_, ra_), (pg, pg_r, sg_, rg_)):
                    for k in range(KC):
                        self.mm(pp[:, :n], pr, sl[:, k, :], self.uall[:, k, t0:t0 + n], k == 0, k == KC - 1,
                                reads=(rr, self.uall_r))
                b = ci % 2
                ci += 1
                self.act(sgm[b][:, :n], pg[:, :n], AF.Sigmoid, reads=(pg_r, self.vec_res), writes=(sgm_r[b],),
                         bias=b1[:, 16 + m:17 + m])
                dstv = zb[:, 15 + t0:15 + t0 + n] if v == 0 else zc[:, 15:15 + n]
                self.stt(dstv, pa[:, :n], b1[:, m:m + 1], sgm[b][:, :n], ALU.add, ALU.mult,
                         reads=(pa_r, sgm_r[b], self.vec_res), writes=(z_r,))
            ab = m % 2
            for ti, (t0, n, v) in enumerate(TILES):
                src, s0 = (zb, t0) if v == 0 else (zc, 0)
                pc, pc_r = self.ps()
                for k in range(31):
                    self.mm(pc[:, :n], pc_r, dg[:, k, :], src[:, s0 + k:s0 + k + n], k == 0, k == 30, reads=(dg_r, z_r))
                self.act(acc[ab][:, t0:t0 + n], pc[:, :n], AF.Identity, reads=(pc_r, self.vec_res), writes=(acc_r[ab],),
                         bias=bdw[:, m:m + 1])
            self.store(self.CT[m * 128:(m + 1) * 128, :], acc[ab], acc_r[ab], writes=(self.ct_res[m],))
        self.areset()
        self.xres_setup()
        self.norm_setup()
        rb = [self.aalloc(KC * 512, BF16).rearrange("p (k t) -> p k t", t=512) for _ in range(2)]
        rb_r = [P.res(), P.res()]
        cs = self.aalloc(KC * 512).rearrange("p (k t) -> p k t", t=512)
        cs_r = P.res(dma=True)
        cb16 = [self.aalloc(512, BF16), self.aalloc(512, BF16)]
        cb16_r = [P.res(), P.res()]
        mean = self.aalloc(512)
        mean_r = P.res()
        msq = self.aalloc(512)
        msq_r = P.res()

        def rhs_fn(grp):
            for s, ti in enumerate(grp):
                t0, n, v = TILES[ti]
                self.load(cs[:, :, :n], self.CT[:, t0:t0 + n].rearrange("(k p) t -> p k t", p=128), cs_r,
                          reads=tuple(self.ct_res))
                p1, p1_r = self.ps()
                p2, p2_r = self.ps()
                for k in range(KC):
                    b = k % 2
                    self.copy(cb16[b][:, :n], cs[:, k, :n], reads=(cs_r,), writes=(cb16_r[b],), eng="dve")
                    self.mm(p1[:, :n], p1_r, self.ones_bf, cb16[b][:, :n], k == 0, k == KC - 1, reads=(cb16_r[b], self.ones_res))
                    self.act(self.sq[b][:, :n], cs[:, k, :n], AF.Square, reads=(cs_r,), writes=(self.sq_r[b],))
                    self.mm(p2[:, :n], p2_r, self.ones_bf, self.sq[b][:, :n], k == 0, k == KC - 1, reads=(self.sq_r[b], self.ones_res))
                self.ts(mean[:, :n], p1[:, :n], 1.0 / D, ALU.mult, reads=(p1_r,), writes=(mean_r,))
                self.tt(msq[:, :n], mean[:, :n], mean[:, :n], ALU.mult, reads=(mean_r,), writes=(msq_r,))
                self.stt(msq[:, :n], p2[:, :n], 1.0 / D, msq[:, :n], ALU.mult, ALU.subtract, reads=(p2_r,), writes=(msq_r,))
                self.ts(msq[:, :n], msq[:, :n], 0.0, ALU.max, reads=(), writes=(msq_r,))
                self.act(self.srt[:, :n], msq[:, :n], AF.Sqrt, reads=(msq_r,), writes=(self.srt_r,), bias=self.eps_ap)
                P.op("dve", (lambda e, o=self.rstd[:, :n], i_=self.srt[:, :n]: e.reciprocal(out=o, in_=i_)),
                     reads=(self.srt_r,), writes=(self.rstd_r,))
                for k in range(KC):
                    b = k % 2
                    self.tt(self.tmpn[b][:, :n], cs[:, k, :n], mean[:, :n], ALU.subtract, reads=(cs_r, mean_r), writes=(self.tmpn_r[b],))
                    self.tt(self.tmpn[b][:, :n], self.tmpn[b][:, :n], self.rstd[:, :n], ALU.mult, reads=(self.rstd_r,), writes=(self.tmpn_r[b],))
                    self.act(rb[s][:, k, :n], self.tmpn[b][:, :n], AF.Silu, reads=(self.tmpn_r[b], self.vec_res), writes=(rb_r[s],),
                             bias=lnb[:, k:k + 1], scale=lng[:, k:k + 1])
            return (lambda s, j, n: rb[s][:, j, :n]), (rb_r[0], rb_r[1])

        self.out_proj(self.cv_w_pw2, KC, rhs_fn, [[0, 1], [2, 3], [4, 5]], gates, bias=b2g)

    def run_units(self, units):
        n = len(units)
        for k in range(n + 1):
            if k < n:
                pre, A, B, post = units[k]
                if pre is not None:
                    pre()
                A(k % 2)
            if k >= 1:
                pre, A, B, post = units[k - 1]
                B((k - 1) % 2)
                if post is not None:
                    post()

    def proj_T(self, W, col0, nchunks, DST, dst_res, st, st_r, evac):
        for m in range(nchunks):
            slab, sres = self.wget(W[:, col0 + m * 128:col0 + (m + 1) * 128], KC, 128)
            b = m % 2
            for ti, (t0, n, v) in enumerate(TILES):
                if n == 0:
                    continue
                ps, psr = self.ps()
                for k in range(KC):
                    self.mm(ps[:, :n], psr, slab[:, k, :], self.uall[:, k, t0:t0 + n], k == 0, k == KC - 1,
                            reads=(sres, self.uall_r))
                evac(m, ti, ps, psr, st[b][:, t0:t0 + n], st_r[b])
            self.store(DST[m * 128:(m + 1) * 128, :], st[b], st_r[b], writes=(dst_res[m],))

    def proj_V(self, W, col0, ncols, vst, vst_r, blocks=None):
        i = 0
        if blocks is None:
            blocks = list(range(NBLK))
        for c0 in range(0, ncols, 256):
            slab, sres = self.wget(W[:, col0 + c0:col0 + c0 + 256], KC, 256)
            for blk in blocks:
                ps, psr = self.ps()
                for k in range(KC):
                    self.mm(ps[:, :256], psr, self.uall[:, k, blk * 128:(blk + 1) * 128], slab[:, k, :], k == 0, k == KC - 1,
                            reads=(sres, self.uall_r))
                b = i % 2
                i += 1
                self.copy(vst[b], ps[:, :256], reads=(psr,), writes=(vst_r[b],), eng=("act" if b else "dve"))
                self.store(self.VV[blk * 128:(blk + 1) * 128, c0:c0 + 256], vst[b], vst_r[b], writes=(self.vv_res[blk],))

    def mixer_na(self, i):
        P = self.P
        self.areset()
        gates = self.mixer_u(i)
        st = [self.aalloc(TT, BF16), self.aalloc(TT, BF16)]
        st_r = [P.res(dma=True), P.res(dma=True)]
        vst = [self.aalloc(256, BF16), self.aalloc(256, BF16)]
        vst_r = [P.res(dma=True), P.res(dma=True)]
        cnt = [0]

        def evac(m, ti, ps, psr, dst, dst_r):
            n = TILES[ti][1]
            cnt[0] += 1
            self.copy(dst, ps[:, :n], reads=(psr,), writes=(dst_r,), eng=("act" if cnt[0] % 2 else "dve"))

        W = self.na_w_qkv
        self.proj_T(W, 0, 16, self.QT, self.qt_res, st, st_r, evac)
        self.proj_T(W, D, 16, self.KT, self.kt_res, st, st_r, evac)
        self.proj_V(W, 2 * D, D, vst, vst_r)
        self.areset()
        qh = [self.aalloc(TT, BF16) for _ in range(2)]
        kh = [self.aalloc(TT, BF16) for _ in range(2)]
        vh = [self.aalloc(NBLK * 128, BF16).rearrange("p (b d) -> p b d", d=128) for _ in range(2)]
        bia = [self.aalloc(25 * 128) for _ in range(2)]
        hr = [[P.res(dma=True) for _ in range(4)] for _ in range(2)]
        ost = [self.aalloc(TT, BF16) for _ in range(2)]
        ost_r = [P.res(dma=True), P.res(dma=True)]
        sb = [self.aalloc(640) for _ in range(2)]
        sb_r = [P.res(), P.res()]
        pT = [self.aalloc(896, BF16) for _ in range(2)]
        pT_r = [P.res(), P.res()]
        rec = [self.aalloc(256) for _ in range(2)]
        rec_r = [P.res(), P.res()]
        lnb = [self.aalloc(256) for _ in range(2)]
        lnb_r = [P.res(), P.res()]

        def na_unit(h, hb, p):
            bs = min(max(2 * p - 4, 0), 30)
            var = {0: 0, 1: 1, 18: 3, 19: 4}.get(p, 2)
            q0 = p * 128
            ktoks = [bs * 64 + c * 128 for c in range(5)] + [WIN, WIN + 128]
            vblk = [t // 128 for t in ktoks]

            def A(u):
                ps1, ps1_r = self.ps()
                ps2, ps2_r = self.ps()
                for c in range(7):
                    pp, pr = (ps1, ps1_r) if c < 4 else (ps2, ps2_r)
                    cc = c % 4
                    self.mm(pp[:, cc * 128:(cc + 1) * 128], pr, kh[hb][:, ktoks[c]:ktoks[c] + 128], qh[hb][:, q0:q0 + 128],
                            True, True, reads=(hr[hb][0], hr[hb][1]))
                bv = bia[hb][:, var * 640:(var + 1) * 640]
                self.stt(sb[u][:, 0:512], ps1, SCALE, bv[:, 0:512], ALU.mult, ALU.add,
                         reads=(ps1_r, hr[hb][3]), writes=(sb_r[u],))
                self.stt(sb[u][:, 512:640], ps2[:, 0:128], SCALE, bv[:, 512:640], ALU.mult, ALU.add,
                         reads=(ps2_r, hr[hb][3]), writes=(sb_r[u],))
                self.act(pT[u][:, 0:640], sb[u], AF.Exp, reads=(sb_r[u],), writes=(pT_r[u],))
                self.act(pT[u][:, 640:896], ps2[:, 128:384], AF.Exp, reads=(ps2_r,), writes=(pT_r[u],), scale=SCALE)

            def B(u):
                po, po_r = self.ps()
                pz, pz_r = self.ps()
                for c in range(7):
                    self.mm(po[:, 0:128], po_r, vh[hb][:, vblk[c], :], pT[u][:, c * 128:(c + 1) * 128], c == 0, c == 6,
                            reads=(hr[hb][2], pT_r[u]))
                for c in range(7):
                    self.mm(pz[:, 0:128], pz_r, self.ones_bf, pT[u][:, c * 128:(c + 1) * 128], c == 0, c == 6,
                            reads=(self.ones_res, pT_r[u]))
                self.act(lnb[u][:, 0:128], pz[:, 0:128], AF.Ln, reads=(pz_r,), writes=(lnb_r[u],))
                self.act(rec[u][:, 0:128], lnb[u][:, 0:128], AF.Exp, reads=(lnb_r[u],), writes=(rec_r[u],), scale=-1.0)
                self.tt(ost[hb][:, q0:q0 + 128], po[:, 0:128], rec[u][:, 0:128], ALU.mult,
                        reads=(po_r, rec_r[u]), writes=(ost_r[hb],))
            return A, B

        def na_ctx_unit(h, hb):
            def A(u):
                ps1, ps1_r = self.ps()
                for c in range(2):
                    self.mm(ps1[:, c * 256:(c + 1) * 256], ps1_r, kh[hb][:, WIN + c * 128:WIN + (c + 1) * 128], qh[hb][:, WIN:TT],
                            True, True, reads=(hr[hb][0], hr[hb][1]))
                self.act(pT[u][:, 0:512], ps1, AF.Exp, reads=(ps1_r,), writes=(pT_r[u],), scale=SCALE)

            def B(u):
                po, po_r = self.ps()
                pz, pz_r = self.ps()
                for c in range(2):
                    self.mm(po[:, 0:256], po_r, vh[hb][:, 20 + c, :], pT[u][:, c * 256:(c + 1) * 256], c == 0, c == 1,
                            reads=(hr[hb][2], pT_r[u]))
                for c in range(2):
                    self.mm(pz[:, 0:256], pz_r, self.ones_bf, pT[u][:, c * 256:(c + 1) * 256], c == 0, c == 1,
                            reads=(self.ones_res, pT_r[u]))
                self.act(lnb[u], pz[:, 0:256], AF.Ln, reads=(pz_r,), writes=(lnb_r[u],))
                self.act(rec[u], lnb[u], AF.Exp, reads=(lnb_r[u],), writes=(rec_r[u],), scale=-1.0)
                self.tt(ost[hb][:, WIN:TT], po[:, 0:256], rec[u], ALU.mult, reads=(po_r, rec_r[u]), writes=(ost_r[hb],))
            return A, B

        def na_pre(h, hb):
            def f():
                self.load(qh[hb], self.QT[h * 128:(h + 1) * 128, :], hr[hb][0], reads=(self.qt_res[h],))
                self.load(kh[hb], self.KT[h * 128:(h + 1) * 128, :], hr[hb][1], reads=(self.kt_res[h],))
                self.load(vh[hb], self.VV[:, h * 128:(h + 1) * 128].rearrange("(b p) d -> p b d", p=128), hr[hb][2],
                          reads=tuple(self.vv_res))
                self.load(bia[hb], self.na_bias[h], hr[hb][3])
            return f

        def na_post(h, hb):
            def f():
                self.store(self.ST[h * 128:(h + 1) * 128, :], ost[hb], ost_r[hb], writes=(self.st_res[h],))
            return f

        units = []
        for h in range(16):
            hb = h % 2
            for p in range(18):
                A, B = na_unit(h, hb, p)
                units.append((na_pre(h, hb) if p == 0 else None, A, B, None))
            A, B = na_ctx_unit(h, hb)
            units.append((None, A, B, na_post(h, hb)))
        self.run_units(units)
        TILES[4] = (2048, 192, 0)
        self.areset()
        self.xres_setup()
        rb = [self.aalloc(KC * 512, BF16).rearrange("p (k t) -> p k t", t=512) for _ in range(2)]
        rb_r = [P.res(dma=True), P.res(dma=True)]
        self.out_proj(self.na_w_o, KC, self.rhs_from_dram(self.ST, self.st_res, rb, rb_r), [[0, 1], [2, 3], [4, 5]], gates)

    def mixer_swa(self, i):
        P = self.P
        self.areset()
        gates = self.mixer_u(i)
        st = [self.aalloc(TT, BF16), self.aalloc(TT, BF16)]
        st_r = [P.res(dma=True), P.res(dma=True)]
        vst = [self.aalloc(256, BF16), self.aalloc(256, BF16)]
        vst_r = [P.res(dma=True), P.res(dma=True)]
        cosT = self.aalloc(WIN)
        sinT = self.aalloc(WIN)
        tab_r = P.res(dma=True)
        self.load(cosT, self.rope_cos, tab_r)
        self.load(sinT, self.rope_sin, tab_r)
        t1 = [self.aalloc(512), self.aalloc(512)]
        t1_r = [P.res(), P.res()]
        t2 = [self.aalloc(512), self.aalloc(512)]
        t2_r = [P.res(), P.res()]
        cnt = [0]
        W = self.swa_w_qkv
        Wsw = self.swa_w_sw

        def rope_proj(col0, col0sw, nchunks, DST, dst_res, tiles):
            for m in range(nchunks):
                slab, sres = self.wget(W[:, col0 + m * 128:col0 + (m + 1) * 128], KC, 128)
                slab2, sres2 = self.wget(Wsw[:, col0sw + m * 128:col0sw + (m + 1) * 128], KC, 128)
                b = m % 2
                for ti in tiles:
                    t0, n, v = TILES[ti]
                    ps, psr = self.ps()
                    for k in range(KC):
                        self.mm(ps[:, :n], psr, slab[:, k, :], self.uall[:, k, t0:t0 + n], k == 0, k == KC - 1,
                                reads=(sres, self.uall_r))
                    if v == 1:
                        self.copy(st[b][:, t0:t0 + n], ps[:, :n], reads=(psr,), writes=(st_r[b],), eng="act")
                        continue
                    ps2, ps2r = self.ps()
                    for k in range(KC):
                        self.mm(ps2[:, :n], ps2r, slab2[:, k, :], self.uall[:, k, t0:t0 + n], k == 0, k == KC - 1,
                                reads=(sres2, self.uall_r))
                    c = cnt[0] % 2
                    cnt[0] += 1
                    self.tt(t1[c][:, :n], ps[:, :n], cosT[:, t0:t0 + n], ALU.mult, reads=(psr, tab_r), writes=(t1_r[c],))
                    self.tt(t2[c][:, :n], ps2[:, :n], sinT[:, t0:t0 + n], ALU.mult, reads=(ps2r, tab_r), writes=(t2_r[c],))
                    self.tt(st[b][:, t0:t0 + n], t1[c][:, :n], t2[c][:, :n], ALU.add, reads=(t1_r[c], t2_r[c]), writes=(st_r[b],))
                self.store(DST[m * 128:(m + 1) * 128, :], st[b], st_r[b], writes=(dst_res[m],))

        sstop = getattr(self, "swa_stop", 9)
        if sstop <= 1:
            return
        rope_proj(0, 0, 16, self.QT, self.qt_res, (0, 1, 2, 3))
        if sstop <= 2:
            return
        rope_proj(D, D, 4, self.KT, self.kt_res, (0, 1, 2, 3, 4, 5))
        self.proj_V(W, D + 512, 512, vst, vst_r, blocks=list(range(17)) + [20, 21])
        if sstop <= 3:
            return
        self.areset()
        qh = [self.aalloc(TT, BF16) for _ in range(2)]
        kh = [self.aalloc(TT, BF16) for _ in range(2)]
        vh = [self.aalloc(NBLK * 128, BF16).rearrange("p (b d) -> p b d", d=128) for _ in range(2)]
        q_r = [P.res(dma=True), P.res(dma=True)]
        kv_r = [[P.res(dma=True), P.res(dma=True)] for _ in range(2)]
        msk = self.aalloc(256)
        msk_r = P.res(dma=True)
        self.load(msk, self.swa_mask, msk_r)
        esk = self.aalloc(16)
        esk_r = P.res(dma=True)
        esk2_r = P.res()
        self.load(esk, self.swa_sink, esk_r)
        self.act(esk, esk, AF.Exp, reads=(esk_r,), writes=(esk2_r,))
        ost = [self.aalloc(WIN, BF16) for _ in range(2)]
        ost_r = [P.res(dma=True), P.res(dma=True)]
        sb = [self.aalloc(256) for _ in range(2)]
        sb_r = [P.res(), P.res()]
        pT = [self.aalloc(640, BF16) for _ in range(2)]
        pT_r = [P.res(), P.res()]
        rec = [self.aalloc(128) for _ in range(2)]
        rec_r = [P.res(), P.res()]
        den = [self.aalloc(128) for _ in range(2)]
        den_r = [P.res(), P.res()]
        lnb = [self.aalloc(128) for _ in range(2)]
        lnb_r = [P.res(), P.res()]

        def swa_unit(h, hb, kb, qb):
            q0 = qb * 128
            chunks = []
            if qb > 0:
                chunks.append(((qb - 1) * 128, 1))
            chunks.append((qb * 128, 0))
            if qb < 19:
                chunks.append(((qb + 1) * 128, 2))
            chunks += [(WIN, 0), (WIN + 128, 0)]
            nch = len(chunks)

            def A(u):
                ps1, ps1_r = self.ps()
                ps2, ps2_r = self.ps()
                for c, (kt, mk) in enumerate(chunks):
                    pp, pr = (ps1, ps1_r) if c < 4 else (ps2, ps2_r)
                    cc = c % 4
                    self.mm(pp[:, cc * 128:(cc + 1) * 128], pr, kh[kb][:, kt:kt + 128], qh[hb][:, q0:q0 + 128],
                            True, True, reads=(q_r[hb], kv_r[kb][0]))
                mi = 0
                for c, (kt, mk) in enumerate(chunks):
                    pp, pr = (ps1, ps1_r) if c < 4 else (ps2, ps2_r)
                    cc = c % 4
                    src = pp[:, cc * 128:(cc + 1) * 128]
                    if mk == 0:
                        self.act(pT[u][:, c * 128:(c + 1) * 128], src, AF.Exp, reads=(pr,), writes=(pT_r[u],), scale=SCALE)
                    else:
                        mm_ = msk[:, (mk - 1) * 128:mk * 128]
                        sbv = sb[u][:, mi * 128:(mi + 1) * 128]
                        mi += 1
                        self.stt(sbv, src, SCALE, mm_, ALU.mult, ALU.add, reads=(pr, msk_r), writes=(sb_r[u],))
                        self.act(pT[u][:, c * 128:(c + 1) * 128], sbv, AF.Exp, reads=(sb_r[u],), writes=(pT_r[u],))

            def B(u):
                po, po_r = self.ps()
                pz, pz_r = self.ps()
                for c, (kt, mk) in enumerate(chunks):
                    self.mm(po[:, 0:128], po_r, vh[kb][:, kt // 128, :], pT[u][:, c * 128:(c + 1) * 128], c == 0, c == nch - 1,
                            reads=(kv_r[kb][1], pT_r[u]))
                for c in range(nch):
                    self.mm(pz[:, 0:128], pz_r, self.ones_bf, pT[u][:, c * 128:(c + 1) * 128], c == 0, c == nch - 1,
                            reads=(self.ones_res, pT_r[u]))
                self.act(lnb[u], pz[:, 0:128], AF.Ln, reads=(pz_r, esk2_r), writes=(lnb_r[u],), bias=esk[:, h:h + 1])
                self.act(rec[u], lnb[u], AF.Exp, reads=(lnb_r[u],), writes=(rec_r[u],), scale=-1.0)
                self.tt(ost[hb][:, q0:q0 + 128], po[:, 0:128], rec[u], ALU.mult, reads=(po_r, rec_r[u]), writes=(ost_r[hb],))
            return A, B

        def swa_pre(h, hb, kb, kv):
            def f():
                self.load(qh[hb], self.QT[h * 128:(h + 1) * 128, :], q_r[hb], reads=(self.qt_res[h],))
                if h % 4 == 0:
                    self.load(kh[kb], self.KT[kv * 128:(kv + 1) * 128, :], kv_r[kb][0], reads=(self.kt_res[kv],))
                    self.load(vh[kb], self.VV[:, kv * 128:(kv + 1) * 128].rearrange("(b p) d -> p b d", p=128), kv_r[kb][1],
                              reads=tuple(self.vv_res))
            return f

        def swa_post(h, hb):
            def f():
                self.store(self.ST[h * 128:(h + 1) * 128, 0:WIN], ost[hb], ost_r[hb], writes=(self.st_res[h],))
            return f

        units = []
        for h in range(16):
            hb = h % 2
            kv = h // 4
            kb = kv % 2
            for qb in range(16):
                A, B = swa_unit(h, hb, kb, qb)
                units.append((swa_pre(h, hb, kb, kv) if qb == 0 else None, A, B, swa_post(h, hb) if qb == 15 else None))
        self.run_units(units)
        if sstop <= 4:
            return
        self.areset()
        self.xres_setup()
        rb = [self.aalloc(KC * 512, BF16).rearrange("p (k t) -> p k t", t=512) for _ in range(2)]
        rb_r = [P.res(dma=True), P.res(dma=True)]
        self.out_proj(self.swa_w_o, KC, self.rhs_from_dram(self.ST, self.st_res, rb, rb_r), [[0, 1], [2, 3]], gates)

    def program(self):
        self.pers_off = 0
        self.arena_off = 0
        self.ps_i = 0
        self.ws_idx = 0
        self.eps_ap = self.palloc(16)[:, 0:1]
        self.eps_res = self.P.res()
        self.P.op("dve", (lambda e: e.memset(self.eps_ap, EPS)), writes=(self.eps_res,))
        self.setup()
        if self.lo == 0:
            self.transpose_in()
        else:
            r0 = self.P.res(dma=True, persistent=True)
            self.P.dma("sp", (lambda e: e.dma_start(out=self.XT, in_=self.xt_in)), r0, reads=(), writes=tuple(self.xt_res) + (r0,))
            self.P.barrier()
        stop = self.stop_after
        done = False
        TILES[4] = (2048, 512, 0)
        for i in range(self.lo, self.hi):
            last = i == 3
            if last:
                TILES[4] = (2048, 128, 0)
            self.mod_phase(i)
            for ph in ("a", "b", "c"):
                if ph == "a":
                    self.ffn_phase(i, 0, True)
                elif ph == "b":
                    [self.mixer_na, self.mixer_conformer, self.mixer_sconv, self.mixer_swa][i](i)
                else:
                    self.ffn_phase(i, 1, not last)
                self.snap("%d%s" % (i, ph))
                if stop == "%d%s" % (i, ph):
                    done = True
                    break
            if done:
                break
        if self.hi == 4:
            self.final_phase()
        else:
            self.P.barrier()
            r1 = self.P.res(dma=True, persistent=True)
            self.P.dma("sp", (lambda e: e.dma_start(out=self.xt_out, in_=self.XT)), r1, reads=tuple(self.xt_res), writes=(r1,))
        self.P.barrier()
        if self.dbg:
            r = self.P.res(dma=True)
            self.P.dma("sp", (lambda e: e.dma_start(out=self.dbg_xt, in_=self.XT)), r,
                       reads=tuple(self.xt_res), writes=(r,))
        self.P.barrier()

    def build(self):
        from contextlib import ExitStack
        self.P.dry = True
        self.program()
        self.P.dry = False
        self.program()
        with ExitStack() as es:
            self.P.emit(self.nc, es)
        return self.nc


def _na_bias_table(rpb, par):
    H = 16
    tab = np.full((H, 5, 640, 128), NEG, np.float32)
    plist = [0, 1, 2, 18, 19]
    for vi, p in enumerate(plist):
        bs = min(max(2 * p - 4, 0), 30)
        for a in range(2):
            lr = 2 * p + a
            rg = lr if par == 0 else 63 - lr
            rs = min(max(rg - 4, 0), 56)
            for j in range(8):
                krow_g = rs + j
                klr = krow_g if par == 0 else 63 - krow_g
                jb = klr - bs
                if not (0 <= jb < 10):
                    continue
                roff = krow_g - rg + 7
                for qc in range(64):
                    cg = qc if par == 0 else 63 - qc
                    cs = min(max(cg - 8, 0), 48)
                    kc_g = cs + np.arange(16)
                    coff = kc_g - cg + 15
                    klc = kc_g if par == 0 else 63 - kc_g
                    tab[:, vi, jb * 64 + klc, a * 64 + qc] = rpb[:, roff, coff]
    t = tab.reshape(H, 5, 5, 128, 128).transpose(0, 3, 1, 2, 4)
    return np.ascontiguousarray(t.reshape(H, 128, 25 * 128))


def _rope_tables(par):
    inv_freq = np.power(np.float32(10000.0), -np.arange(32, dtype=np.float32) / np.float32(32)).astype(np.float32)
    pos = np.arange(WIN) if par == 0 else 4095 - np.arange(WIN)
    row = (pos // GRID_W).astype(np.float32)
    colp = (pos % GRID_W).astype(np.float32)
    cos = np.zeros((128, WIN), np.float32)
    sin = np.zeros((128, WIN), np.float32)
    for d in range(128):
        half = d // 64
        e = d % 64
        f = e % 32
        ang = (row if half == 0 else colp) * inv_freq[f]
        cos[d] = np.cos(ang)
        sin[d] = -np.sin(ang) if e < 32 else np.sin(ang)
    return cos, sin


def _sw_perm():
    d = np.arange(128)
    e = d % 64
    return np.where(e < 32, d + 32, d - 32)


_NC_CACHE = {}


def _get_nc(lo, hi):
    key = (lo, hi)
    if key not in _NC_CACHE:
        _NC_CACHE[key] = Builder(None, False, lo, hi).build()
    return _NC_CACHE[key]


def _host_inputs(inputs, cores, lo=0, hi=4, xts=None):
    f = lambda a: np.ascontiguousarray(np.asarray(a, dtype=np.float32))
    x, c, ctx, c_ctx = (f(inputs[k]) for k in ("x", "c", "ctx", "c_ctx"))
    perm = _sw_perm()
    wq = f(inputs["swa_w_qkv"][0])
    qcols = (np.arange(16)[:, None] * 128 + perm[None, :]).reshape(-1)
    kcols = 2048 + (np.arange(4)[:, None] * 128 + perm[None, :]).reshape(-1)
    w_sw = np.ascontiguousarray(wq[:, np.concatenate([qcols, kcols])])
    i_ = np.arange(128)
    m_lo = np.where(i_[:, None] >= i_[None, :], 0.0, NEG).astype(np.float32)
    m_hi = np.where(i_[:, None] <= i_[None, :], 0.0, NEG).astype(np.float32)
    shared = {
        "w_mod": f(inputs["w_mod"][lo:hi]), "b_mod": f(inputs["b_mod"][lo:hi]), "norm_g": f(inputs["norm_g"][lo:hi]),
        "ffn_w_in": f(inputs["ffn_w_in"][lo:hi]), "ffn_w_out": f(inputs["ffn_w_out"][lo:hi]),
        "na_w_qkv": f(inputs["na_w_qkv"][0]), "na_w_o": f(inputs["na_w_o"][0]),
        "cv_w_pw1": f(inputs["cv_w_pw1"][0]), "cv_b_pw1": f(inputs["cv_b_pw1"][0]),
        "cv_b_dw": f(inputs["cv_b_dw"][0]), "cv_ln_g": f(inputs["cv_ln_g"][0]), "cv_ln_b": f(inputs["cv_ln_b"][0]),
        "cv_w_pw2": f(inputs["cv_w_pw2"][0]), "cv_b_pw2": f(inputs["cv_b_pw2"][0]),
        "sc_w_in": f(inputs["sc_w_in"][0]), "sc_w_out": f(inputs["sc_w_out"][0]),
        "swa_w_qkv": wq, "swa_w_sw": w_sw, "swa_w_o": f(inputs["swa_w_o"][0]), "swa_sink": np.ascontiguousarray(np.tile(f(inputs["swa_sink"][0])[None, :], (128, 1))),
        "final_g": f(inputs["final_g"]), "ident": np.eye(128, dtype=np.float32),
        "swa_mask": np.ascontiguousarray(np.concatenate([m_lo, m_hi], axis=1)),
    }
    ropes = [_rope_tables(0), _rope_tables(1)]
    rpb = f(inputs["na_rpb"][0])
    nab = [_na_bias_table(rpb, 0), _na_bias_table(rpb, 1)]
    wdw = f(inputs["cv_w_dw"][0])
    wsc = f(inputs["sc_w_conv"][0])
    wdws = [wdw, np.ascontiguousarray(wdw[::-1])]
    wscs = [wsc, np.ascontiguousarray(wsc[::-1])]
    in_maps = []
    for core in cores:
        b, half = core // 2, core % 2
        m = dict(shared)
        if half == 0:
            m["x_win"] = np.ascontiguousarray(x[b, 0:WIN])
            m["ctx_in"] = np.ascontiguousarray(ctx[b])
        else:
            m["x_win"] = np.ascontiguousarray(x[b, ::-1][0:WIN])
            m["ctx_in"] = np.ascontiguousarray(ctx[b, ::-1])
        m["cvec"] = np.ascontiguousarray(np.stack([c[b], c_ctx]))
        m["na_bias"] = nab[half]
        m["cv_w_dw"] = wdws[half]
        m["sc_w_conv"] = wscs[half]
        m["rope_cos"], m["rope_sin"] = ropes[half]
        if xts is not None:
            m["xt_in"] = xts[len(in_maps)]
        in_maps.append(m)
    return in_maps


def kernel(**inputs):
    cores = list(range(8))
    nc = _get_nc(0, 4)
    res = run_bass_kernel_spmd(nc, _host_inputs(inputs, cores, 0, 4), core_ids=cores)
    out = np.empty((4, 4096, D), np.float32)
    for core in range(8):
        b, half = core // 2, core % 2
        o = res.results[core]["out"]
        if half == 0:
            out[b, 0:2048] = o
        else:
            out[b, 2048:4096] = o[::-1]
    return out
```
